# Optimizing a Trainium2 kernel written in Bass

```python
import jax, jax.numpy as jnp
from jax import lax
import numpy as np

D_MODEL = 1024
BATCH = 4
SEQ = 8192
DEPTH = 1
DEC_BATCH = 32
DEC_SEQ = 64
PAST_LEN = 1024

CHUNK = 64
D_RNN = D_MODEL
N_LRU_BLOCKS = 16
LRU_BLOCK = D_RNN // N_LRU_BLOCKS
LRU_C = 8.0
CONV_W = 4
HEAD_DIM = 64
N_HEADS = D_MODEL // HEAD_DIM
N_KV_HEADS = 4
GROUP = N_HEADS // N_KV_HEADS
Q_W = N_HEADS * HEAD_DIM
KV_W = N_KV_HEADS * HEAD_DIM
ROT_DIM = HEAD_DIM // 4
ROPE_THETA = 500000.0
WINDOW = 128
WINDOW_CHUNKS = WINDOW // CHUNK
D_FF = 2816
FFN_CONV_W = 3
D_IN = D_RNN + D_RNN + Q_W + 2 * KV_W + D_RNN + Q_W
RMS_EPS = 1e-6
NEG_INF = -1e30
ATTN_CACHE = min(WINDOW, PAST_LEN)

kernel_name = "griffin_swa_sink_convffn_stream_step"


def rmsnorm(x, g):
    x32 = x.astype(jnp.float32)
    y = x32 * lax.rsqrt(jnp.mean(x32 * x32, axis=-1, keepdims=True) + RMS_EPS)
    return (y * g.astype(jnp.float32)).astype(x.dtype)


def causal_dwconv(x, buf, w, b):
    width = w.shape[0]
    t = x.shape[1]
    xp = jnp.concatenate([buf.astype(x.dtype), x], axis=1)
    y = b.astype(x.dtype) + sum(w[j].astype(x.dtype) * xp[:, j:j + t] for j in range(width))
    return y, xp[:, -(width - 1):]


def rope_partial(x, pos):
    half = ROT_DIM // 2
    inv_freq = jnp.power(ROPE_THETA, -jnp.arange(half, dtype=jnp.float32) * (2.0 / ROT_DIM))
    ang = pos.astype(jnp.float32)[:, None] * inv_freq[None, :]
    cos = jnp.cos(ang)[None, :, None, :]
    sin = jnp.sin(ang)[None, :, None, :]
    x32 = x.astype(jnp.float32)
    x1 = x32[..., :half]
    x2 = x32[..., half:ROT_DIM]
    out = jnp.concatenate([x1 * cos - x2 * sin, x2 * cos + x1 * sin, x32[..., ROT_DIM:]], axis=-1)
    return out.astype(x.dtype)


def sink_softmax(logits, sink):
    m = jnp.maximum(jnp.max(logits, axis=-1, keepdims=True), sink)
    e = jnp.exp(logits - m)
    return e / (jnp.sum(e, axis=-1, keepdims=True) + jnp.exp(sink - m))


def band_attention(q, k, v, sinks):
    b, s = q.shape[:2]
    nc = s // CHUNK
    qc = q.reshape(b, nc, CHUNK, N_KV_HEADS, GROUP, HEAD_DIM)
    pad = ((0, 0), (WINDOW_CHUNKS * CHUNK, 0), (0, 0), (0, 0))
    kp = jnp.pad(k, pad).reshape(b, nc + WINDOW_CHUNKS, CHUNK, N_KV_HEADS, HEAD_DIM)
    vp = jnp.pad(v, pad).reshape(b, nc + WINDOW_CHUNKS, CHUNK, N_KV_HEADS, HEAD_DIM)
    kb = jnp.concatenate([kp[:, j:j + nc] for j in range(WINDOW_CHUNKS + 1)], axis=2)
    vb = jnp.concatenate([vp[:, j:j + nc] for j in range(WINDOW_CHUNKS + 1)], axis=2)
    scores = jnp.einsum('bnqkgd,bnskd->bnkgqs', qc, kb,
                        preferred_element_type=jnp.float32) * (HEAD_DIM ** -0.5)
    key_chunk_off = jnp.repeat(jnp.arange(WINDOW_CHUNKS + 1) - WINDOW_CHUNKS, CHUNK)
    valid = (jnp.arange(nc)[:, None] + key_chunk_off[None, :]) >= 0
    scores = jnp.where(valid[None, :, None, None, None, :], scores, NEG_INF)
    p = sink_softmax(scores, sinks.astype(jnp.float32)[None, None, :, :, None, None])
    o = jnp.einsum('bnkgqs,bnskd->bnqkgd', p.astype(v.dtype), vb)
    return o.reshape(b, s, Q_W)


def cached_attention(q, k, v, k_cache, v_cache, sinks):
    b, t = q.shape[:2]
    kk = jnp.concatenate([k_cache.astype(k.dtype), k], axis=1)
    vv = jnp.concatenate([v_cache.astype(v.dtype), v], axis=1)
    scores = jnp.einsum('btkgd,bskd->bkgts', q, kk,
                        preferred_element_type=jnp.float32) * (HEAD_DIM ** -0.5)
    p = sink_softmax(scores, sinks.astype(jnp.float32)[None, :, :, None, None])
    o = jnp.einsum('bkgts,bskd->btkgd', p.astype(vv.dtype), vv)
    return o.reshape(b, t, Q_W), kk[:, -ATTN_CACHE:], vv[:, -ATTN_CACHE:]


def rg_lru(x, h0, w_a, b_a, w_i, b_i, lam):
    b, t, _ = x.shape
    xb = x.reshape(b, t, N_LRU_BLOCKS, LRU_BLOCK)
    r = jax.nn.sigmoid((jnp.einsum('btnc,ncd->btnd', xb, w_a).reshape(b, t, D_RNN) + b_a).astype(jnp.float32))
    i = jax.nn.sigmoid((jnp.einsum('btnc,ncd->btnd', xb, w_i).reshape(b, t, D_RNN) + b_i).astype(jnp.float32))
    log_a = -LRU_C * r * jax.nn.softplus(-lam.astype(jnp.float32))
    a = jnp.exp(log_a)
    u = jnp.sqrt(-jnp.expm1(2.0 * log_a)) * (i * x.astype(jnp.float32))
    u = u.at[:, 0].add(a[:, 0] * h0.astype(jnp.float32))

    def combine(left, right):
        a_l, u_l = left
        a_r, u_r = right
        return a_l * a_r, a_r * u_l + u_r

    _, h = lax.associative_scan(combine, (a, u), axis=1)
    return h.astype(x.dtype), h[:, -1].astype(x.dtype)


def trunk(x, pos, cache_k, cache_v, lru_h, lru_conv, ffn_conv,
          norm_mix, w_in, lru_conv_w, lru_conv_b, lru_gate_a_w, lru_gate_a_b,
          lru_gate_i_w, lru_gate_i_b, lru_lambda, attn_sinks, w_out,
          norm_ffn, w_up, ffn_conv_w, ffn_conv_b, w_down, norm_final):
    b, t, _ = x.shape
    first = cache_k is None
    cuts = np.cumsum([D_RNN, D_RNN, Q_W, KV_W, KV_W, D_RNN]).tolist()
    new_k, new_v, new_h, new_lc, new_fc = [], [], [], [], []
    for l in range(DEPTH):
        xn = rmsnorm(x, norm_mix[l])
        proj = xn @ w_in[l]
        x_rnn, y_gate, q, k, v, g_rnn, g_attn = jnp.split(proj, cuts, axis=-1)
        if first:
            h0 = jnp.zeros((b, D_RNN), x.dtype)
            cbuf = jnp.zeros((b, CONV_W - 1, D_RNN), x.dtype)
        else:
            h0 = lru_h[l]
            cbuf = lru_conv[l]
        xc, cbuf_new = causal_dwconv(x_rnn, cbuf, lru_conv_w[l], lru_conv_b[l])
        h, h_last = rg_lru(xc, h0, lru_gate_a_w[l], lru_gate_a_b[l],
                           lru_gate_i_w[l], lru_gate_i_b[l], lru_lambda[l])
        y_rnn = h * jax.nn.gelu(y_gate)
        qh = rope_partial(q.reshape(b, t, N_HEADS, HEAD_DIM), pos).reshape(b, t, N_KV_HEADS, GROUP, HEAD_DIM)
        kh = rope_partial(k.reshape(b, t, N_KV_HEADS, HEAD_DIM), pos)
        vh = v.reshape(b, t, N_KV_HEADS, HEAD_DIM)
        sinks = attn_sinks[l].reshape(N_KV_HEADS, GROUP)
        if first:
            y_attn = band_attention(qh, kh, vh, sinks)
            k_keep = kh[:, -ATTN_CACHE:]
            v_keep = vh[:, -ATTN_CACHE:]
        else:
            y_attn, k_keep, v_keep = cached_attention(qh, kh, vh, cache_k[l], cache_v[l], sinks)
        mixed = jax.nn.sigmoid(g_rnn) * y_rnn + jax.nn.sigmoid(g_attn) * y_attn
        x = x + mixed @ w_out[l]
        xn2 = rmsnorm(x, norm_ffn[l])
        up = xn2 @ w_up[l]
        fbuf = jnp.zeros((b, FFN_CONV_W - 1, 2 * D_FF), x.dtype) if first else ffn_conv[l]
        up_c, fbuf_new = causal_dwconv(up, fbuf, ffn_conv_w[l], ffn_conv_b[l])
        gate, val = jnp.split(up_c, 2, axis=-1)
        x = x + (jax.nn.silu(gate) * val) @ w_down[l]
        new_k.append(k_keep)
        new_v.append(v_keep)
        new_h.append(h_last)
        new_lc.append(cbuf_new)
        new_fc.append(fbuf_new)
    y = rmsnorm(x, norm_final)
    return (y, jnp.stack(new_k), jnp.stack(new_v), jnp.stack(new_h),
            jnp.stack(new_lc), jnp.stack(new_fc))


def setup_inputs(seed: int = 0) -> dict:
    key = jax.random.key(seed)
    ks = jax.random.split(key, 32)
    f32 = jnp.float32

    def nrm(k, shape, scale):
        return jax.random.normal(k, shape, f32) * scale

    u = jax.random.uniform(ks[15], (DEPTH, D_RNN), f32, 0.9, 0.999)
    s = u ** (1.0 / LRU_C)
    lam = jnp.log(s) - jnp.log1p(-s)
    return {
        'x_prompt': nrm(ks[0], (BATCH, SEQ, D_MODEL), 1.0),
        'x_sample': nrm(ks[1], (DEC_BATCH, DEC_SEQ, D_MODEL), 1.0),
        'cache_attn_k': nrm(ks[2], (DEPTH, DEC_BATCH, ATTN_CACHE, N_KV_HEADS, HEAD_DIM), 1.0),
        'cache_attn_v': nrm(ks[3], (DEPTH, DEC_BATCH, ATTN_CACHE, N_KV_HEADS, HEAD_DIM), 1.0),
        'state_lru_h': nrm(ks[4], (DEPTH, DEC_BATCH, D_RNN), 0.5),
        'state_lru_conv': nrm(ks[5], (DEPTH, DEC_BATCH, CONV_W - 1, D_RNN), 0.5),
        'state_ffn_conv': nrm(ks[6], (DEPTH, DEC_BATCH, FFN_CONV_W - 1, 2 * D_FF), 0.5),
        'norm_mix': 1.0 + nrm(ks[7], (DEPTH, D_MODEL), 0.1),
        'w_in': nrm(ks[8], (DEPTH, D_MODEL, D_IN), D_MODEL ** -0.5),
        'lru_conv_w': nrm(ks[9], (DEPTH, CONV_W, D_RNN), CONV_W ** -0.5),
        'lru_conv_b': nrm(ks[10], (DEPTH, D_RNN), 0.02),
        'lru_gate_a_w': nrm(ks[11], (DEPTH, N_LRU_BLOCKS, LRU_BLOCK, LRU_BLOCK), LRU_BLOCK ** -0.5),
        'lru_gate_a_b': nrm(ks[12], (DEPTH, D_RNN), 0.02),
        'lru_gate_i_w': nrm(ks[13], (DEPTH, N_LRU_BLOCKS, LRU_BLOCK, LRU_BLOCK), LRU_BLOCK ** -0.5),
        'lru_gate_i_b': nrm(ks[14], (DEPTH, D_RNN), 0.02),
        'lru_lambda': lam,
        'attn_sinks': nrm(ks[16], (DEPTH, N_HEADS), 0.5),
        'w_out': nrm(ks[17], (DEPTH, D_MODEL, D_MODEL), D_MODEL ** -0.5),
        'norm_ffn': 1.0 + nrm(ks[18], (DEPTH, D_MODEL), 0.1),
        'w_up': nrm(ks[19], (DEPTH, D_MODEL, 2 * D_FF), D_MODEL ** -0.5),
        'ffn_conv_w': nrm(ks[20], (DEPTH, FFN_CONV_W, 2 * D_FF), FFN_CONV_W ** -0.5),
        'ffn_conv_b': nrm(ks[21], (DEPTH, 2 * D_FF), 0.02),
        'w_down': nrm(ks[22], (DEPTH, D_FF, D_MODEL), D_FF ** -0.5),
        'norm_final': 1.0 + nrm(ks[23], (D_MODEL,), 0.1),
    }


def reference(x_prompt, x_sample, cache_attn_k, cache_attn_v, state_lru_h, state_lru_conv,
              state_ffn_conv, norm_mix, w_in, lru_conv_w, lru_conv_b, lru_gate_a_w,
              lru_gate_a_b, lru_gate_i_w, lru_gate_i_b, lru_lambda, attn_sinks, w_out,
              norm_ffn, w_up, ffn_conv_w, ffn_conv_b, w_down, norm_final):
    pos_prompt = jnp.arange(SEQ, dtype=jnp.int32)
    pos_sample = PAST_LEN + jnp.arange(DEC_SEQ, dtype=jnp.int32)
    y_prompt, k_p, v_p, h_p, lc_p, fc_p = trunk(
        x_prompt, pos_prompt, None, None, None, None, None,
        norm_mix, w_in, lru_conv_w, lru_conv_b, lru_gate_a_w, lru_gate_a_b,
        lru_gate_i_w, lru_gate_i_b, lru_lambda, attn_sinks, w_out,
        norm_ffn, w_up, ffn_conv_w, ffn_conv_b, w_down, norm_final)
    y_sample, k_s, v_s, h_s, lc_s, fc_s = trunk(
        x_sample, pos_sample, cache_attn_k, cache_attn_v, state_lru_h, state_lru_conv, state_ffn_conv,
        norm_mix, w_in, lru_conv_w, lru_conv_b, lru_gate_a_w, lru_gate_a_b,
        lru_gate_i_w, lru_gate_i_b, lru_lambda, attn_sinks, w_out,
        norm_ffn, w_up, ffn_conv_w, ffn_conv_b, w_down, norm_final)
    return (y_prompt, y_sample, k_p, v_p, h_p, lc_p, fc_p, k_s, v_s, h_s, lc_s, fc_s)
```

```python
import contextlib
import numpy as np
import concourse.bass as bass
import concourse.mybir as mybir
from concourse.bass_utils import run_bass_kernel_spmd

F32 = mybir.dt.float32
BF16 = mybir.dt.bfloat16
AF = mybir.ActivationFunctionType
ALU = mybir.AluOpType

D = 1024
NJ = 8
SEQ = 8192
NB = 4
NSAMP = 32
LS = 64
PAST = 1024
DFF = 2816
NF = 22
NKV = 4
THETA = 500000.0
EPS = 1e-6
NCORES = 8
SPC = NSAMP // NCORES
MASK_NEG = -30000.0

C_XR, C_YG, C_Q, C_K, C_V, C_GR, C_GA = 0, 1024, 2048, 3072, 3328, 3584, 4608

P_G1 = 0
P_CW = P_G1 + 8
P_CB = P_CW + 32
P_BA = P_CB + 8
P_BI = P_BA + 8
P_LAM = P_BI + 8
P_SINK = P_LAM + 8
P_G2 = P_SINK + 8
P_FW = P_G2 + 8
P_FB = P_FW + 132
P_FLAG = P_FB + 44
P_HB = P_FLAG + 1
NPRM = P_HB + 2

ENG_NAMES = ("pe", "act", "dve", "pool", "sp")
DBG = {}


class Buf:
    __slots__ = ("name", "last_w", "reads", "chan", "dead")

    def __init__(self, name):
        self.name = name
        self.last_w = None
        self.reads = []
        self.chan = None
        self.dead = False

    def renew(self):
        n = Buf(self.name)
        n.last_w, n.reads, n.chan = self.last_w, list(self.reads), self.chan
        self.dead = True
        return n


class Instr:
    __slots__ = ("eng", "fn", "deps", "is_dma", "chan", "tokval", "needs_inc", "idx")


class Sched:
    def __init__(self, nc):
        self.nc = nc
        self.instrs = []
        self.prog = {e: [] for e in ENG_NAMES}
        self.nchan = 0
        self.final_dma = []
        ses = DBG.get("same_eng_sync", True)
        self.same_eng_sync = {"pe": False, "act": ses, "dve": ses, "pool": ses, "sp": False}

    def new_chan(self):
        self.nchan += 1
        return self.nchan - 1

    def cbuf(self, name):
        b = Buf(name)
        b.chan = self.new_chan()
        return b

    def op(self, eng, fn, reads=(), writes=(), dma=False, chan=None):
        deps = set()
        for b in list(reads) + list(writes):
            assert not b.dead, f"use of stale rotating buffer {b.name}"
        for b in reads:
            if b.last_w is not None:
                deps.add(b.last_w)
        for b in writes:
            if b.last_w is not None:
                deps.add(b.last_w)
            deps.update(b.reads)
        if dma and chan is None:
            for b in list(writes) + list(reads):
                if b.chan is not None:
                    chan = b.chan
                    break
            assert chan is not None, "dma without channel"
        ins = Instr()
        ins.eng, ins.fn, ins.is_dma, ins.chan = eng, fn, dma, chan
        ins.tokval, ins.needs_inc = None, False
        ins.idx = len(self.instrs)
        best = {}
        for d in deps:
            dd = self.instrs[d]
            k = ("c", dd.chan) if dd.is_dma else ("e", dd.eng)
            if k not in best or best[k] < d:
                best[k] = d
        ins.deps = set(best.values())
        self.instrs.append(ins)
        self.prog[eng].append(ins)
        for b in reads:
            b.reads.append(ins.idx)
        for b in writes:
            b.last_w = ins.idx
            b.reads = []
        return ins.idx

    def dma(self, eng, out_ap, in_ap, reads=(), writes=(), chan=None, final=False):
        def fn(e, out_ap=out_ap, in_ap=in_ap):
            return e.dma_start(out=out_ap, in_=in_ap)
        i = self.op(eng, fn, reads=reads, writes=writes, dma=True, chan=chan)
        if final:
            self.final_dma.append(i)
        return i

    def emit(self):
        nc = self.nc
        instrs = self.instrs

        def skip(dd, ins):
            return dd.eng == ins.eng and (not dd.is_dma) and (not self.same_eng_sync[ins.eng])

        for ins in instrs:
            for d in ins.deps:
                dd = instrs[d]
                if not skip(dd, ins):
                    dd.needs_inc = True
        for i in self.final_dma:
            instrs[i].needs_inc = True
        eng_cnt = {e: 0 for e in ENG_NAMES}
        chan_cnt = [0] * self.nchan
        for ins in instrs:
            if ins.is_dma:
                chan_cnt[ins.chan] += 1
                ins.tokval = 16 * chan_cnt[ins.chan]
            elif ins.needs_inc:
                eng_cnt[ins.eng] += 1
                ins.tokval = eng_cnt[ins.eng]
        self.eng_cnt = eng_cnt
        with contextlib.ExitStack() as st:
            eng_sem = {e: st.enter_context(nc.semaphore(f"s_{e}")) for e in ENG_NAMES if e != "sp"}
            chan_sem = [st.enter_context(nc.semaphore(f"c_{i}")) for i in range(self.nchan)]
            block = st.enter_context(nc.Block())

            def run(engname, e):
                seen = {}
                for ins in self.prog[engname]:
                    need = {}
                    for d in ins.deps:
                        dd = instrs[d]
                        if skip(dd, ins):
                            continue
                        k = ("c", dd.chan) if dd.is_dma else ("e", dd.eng)
                        if dd.tokval > need.get(k, 0):
                            need[k] = dd.tokval
                    for k, v in need.items():
                        if seen.get(k, 0) >= v:
                            continue
                        seen[k] = v
                        h = chan_sem[k[1]] if k[0] == "c" else eng_sem[k[1]]
                        e.wait_ge(h, v)
                    bi = ins.fn(e)
                    if ins.is_dma:
                        bi.then_inc(chan_sem[ins.chan], 16)
                    elif ins.needs_inc:
                        bi.then_inc(eng_sem[ins.eng], 1)
                if engname == "sp":
                    fin = {}
                    for i in self.final_dma:
                        dd = instrs[i]
                        fin[dd.chan] = max(fin.get(dd.chan, 0), dd.tokval)
                    for c, v in fin.items():
                        e.wait_ge(chan_sem[c], v)

            @block.tensor
            def _(e):
                run("pe", e)

            @block.scalar
            def _(e):
                run("act", e)

            @block.vector
            def _(e):
                run("dve", e)

            @block.gpsimd
            def _(e):
                run("pool", e)

            @block.sync
            def _(e):
                run("sp", e)


class Cfg:
    def __init__(self, half=4096):
        self.HALF = half
        self.WARM = 128
        self.PRE = half - self.WARM
        self.NPOS = 256 + half + LS
        self.TS = SPC * LS


def tile_list(cfg):
    tiles = []
    r = 0
    while r < cfg.PRE:
        T = min(512, cfg.PRE - r)
        tiles.append(dict(mode="pre", src="xpre", row0=r, T=T, nseg=1, L=T, run="p",
                          kv_last=(r + T == cfg.PRE), rope_off=256 - cfg.WARM - 128))
        r += T
    tiles.append(dict(mode="warm", src="xpre", row0=cfg.PRE, T=cfg.WARM, nseg=1, L=cfg.WARM, run="p",
                      rope_off=256 - cfg.WARM, post_flag=True))
    nmain = cfg.HALF // 512
    for i in range(nmain):
        tiles.append(dict(mode="full", src="xmain", row0=i * 512, T=512, nseg=1, L=512, run="p",
                          rope_off=256 + i * 512, first=(i == 0), last=(i == nmain - 1), ydst="y_main"))
    tiles.append(dict(mode="full", src="xs", row0=0, T=cfg.TS, nseg=SPC, L=LS, run="s",
                      rope_off=256 + cfg.HALF, sample=True, last=True, ydst="y_s"))
    return tiles


def kx_early(tiles, i):
    if i < 1 or i >= len(tiles):
        return False
    a, b = tiles[i - 1], tiles[i]
    return (a["mode"] == "full" and b["mode"] == "full" and not a.get("sample") and not b.get("sample"))


def tile_units(t, early=False, next_early=False):
    if t["mode"] == "pre":
        us = [("i", u) for u in range(4)]
        if t.get("kv_last"):
            us += [("i", u) for u in range(4, 8)]
        return us
    us = [] if early else [("i", u) for u in range(4, 8)] + [("i", u) for u in range(4)]
    us += [("i", u) for u in range(12 if early else 8, 16)]
    us += [("i", u) for u in (16, 17, 20, 21, 18, 19, 22, 23)]
    us += [("i", u) for u in range(24, 28)]
    us += [("o", u) for u in range(4)]
    us += [("u", f) for f in range(NF)]
    if t["mode"] == "full":
        for j in range(8):
            us += [("d", 2 * j), ("d", 2 * j + 1)]
            if next_early:
                us.append(("i", 4 + j) if j < 4 else ("i", j - 4))
        if next_early:
            us += [("i", u) for u in range(8, 12)]
    return us


def build_program(cfg):
    nc = bass.Bass("TRN2", target_bir_lowering=False)
    S = Sched(nc)
    HALF = cfg.HALF

    def din(name, shape, dt=F32):
        return nc.dram_tensor(name, list(shape), dt, kind="ExternalInput").ap()

    def dout(name, shape, dt=F32):
        return nc.dram_tensor(name, list(shape), dt, kind="ExternalOutput").ap()

    xsrc = {"xpre": din("xpre", [HALF, D]), "xmain": din("xmain", [HALF, D]), "xs": din("xs", [cfg.TS, D])}
    ropeC = din("ropeC", [128, cfg.NPOS])
    ropeS = din("ropeS", [128, cfg.NPOS])
    prm_d = din("prm", [128, NPRM])
    gF_d = din("gF", [128, D])
    w_in_d = din("w_in_a", [28, 128, 2048])
    w_up_d = din("w_up_a", [NF, 128, 2048])
    w_dn_d = din("w_dn_a", [16, 128, 1408])
    w_out_d = din("w_out_a", [128, 8 * D])
    w_v_d = din("w_v_a", [128, 8 * 256])
    gates_d = din("gates_a", [128, 8 * 2 * 128])
    s_h_d = din("s_h", [128, 8 * SPC])
    s_conv_d = din("s_conv", [128, 8 * SPC * 3])
    s_ffn_d = din("s_ffn", [128, 44 * SPC * 2])
    s_kT_d = din("s_kT", [128, NKV * SPC * 128])
    s_v_d = din("s_v", [SPC, 128, 256])
    s_k_d = din("s_k", [SPC, 128, 256])

    ydst = {"y_main": dout("y_main", [HALF, D]), "y_s": dout("y_s", [cfg.TS, D])}
    okT_p = dout("okT_p", [128, NKV * 128])
    ov_p = dout("ov_p", [128, 256])
    oh_p = dout("oh_p", [128, 8])
    oconv_p = dout("oconv_p", [128, 8 * 3])
    offn_p = dout("offn_p", [128, 44 * 2])
    okT_s = dout("okT_s", [128, NKV * cfg.TS])
    ov_s = dout("ov_s", [cfg.TS, 256])
    okold_s = dout("okold_s", [SPC, 64, 256])
    ovold_s = dout("ovold_s", [SPC, 64, 256])
    oh_s = dout("oh_s", [128, 8 * SPC])
    oconv_s = dout("oconv_s", [128, 8 * SPC * 3])
    offn_s = dout("offn_s", [128, 44 * SPC * 2])

    wi_s = nc.dram_tensor("wi_s", [28, 128, 2048], BF16).ap()
    wu_s = nc.dram_tensor("wu_s", [NF, 128, 2048], BF16).ap()
    wd_s = nc.dram_tensor("wd_s", [16, 128, 1408], BF16).ap()
    wo_s = nc.dram_tensor("wo_s", [4, 128, 2048], BF16).ap()

    tiles = tile_list(cfg)
    units = []
    nt_lim = DBG.get("ntiles", 10 ** 9)
    for i, t in enumerate(tiles):
        units += tile_units(t, early=kx_early(tiles, i), next_early=(kx_early(tiles, i + 1) and i + 1 < nt_lim))

    with contextlib.ExitStack() as st:
        def sb(name, shape, dt):
            return st.enter_context(nc.sbuf_tensor(name, list(shape), dt))

        def ps(name, shape, dt):
            return st.enter_context(nc.psum_tensor(name, list(shape), dt))

        x_sb2 = sb("x_sb2", [128, 2, 4, D], F32)
        acc_all = sb("acc_all", [128, 8, 512], F32)
        xn_tok = sb("xn_tok", [128, 4, D], BF16)
        xnT = sb("xnT", [128, 8, 512], BF16)
        rnn = sb("rnn", [128, 8, 512], F32)
        KT0 = sb("KT0", [128, NKV, 768], BF16)
        KT1 = sb("KT1", [128, NKV, 768], BF16)
        KTz = (KT0, KT1)
        V_all = sb("V_all", [128, 6, 256], BF16)
        V2 = sb("V2", [128, 6, 256], BF16)
        ptown = sb("ptown", [128, 2, 2, 256], BF16)
        R1 = sb("R1", [128, 22 * 512], BF16)
        NTMP = 12
        tmpf = sb("tmpf", [128, NTMP, 520], F32)
        NTB = 4
        tmpb = sb("tmpb", [128, NTB, 512], BF16)
        NTG = 4
        tmpg = sb("tmpg", [128, NTG, 520], F32)
        ystage = sb("ystage", [128, 1, D], F32)
        gates_sb = sb("gates_sb", [128, 8, 2, 128], BF16)
        w_v_sb = sb("w_v_sb", [128, 8, 256], BF16)
        NR = 6
        ring = sb("ring", [128, NR, 2048], BF16)
        prm = sb("prm_sb", [128, NPRM], F32)
        gF = sb("gF_sb", [128, D], F32)
        cst = sb("cst", [128, 64], F32)
        ident_f = sb("ident_f", [128, 128], F32)
        ident_b = sb("ident_b", [128, 128], BF16)
        ones_b = sb("ones_b", [128, 64], BF16)
        stat = sb("stat", [128, 16], F32)
        hst_p = sb("hst_p", [128, 8, 1], F32)
        hst_s = sb("hst_s", [128, 8, SPC], F32)
        convst_p = sb("convst_p", [128, 8, 1, 3], F32)
        convst_s = sb("convst_s", [128, 8, SPC, 3], F32)
        ffnst_p = sb("ffnst_p", [128, 44, 1, 2], F32)
        ffnst_s = sb("ffnst_s", [128, 44, SPC, 2], F32)

        QT = R1[:, 0:8 * 512].rearrange("p (j t) -> p j t", j=8)
        mixedT = R1[:, 8 * 512:16 * 512].rearrange("p (j t) -> p j t", j=8)
        hT = R1[:, 0:22 * 512].rearrange("p (f t) -> p f t", f=NF)
        CS = sb("CS", [128, 2, 512], F32)

        PG = ps("PG", [128, 3, 512], F32)
        PH = ps("PH", [128, 4, 512], F32)
        TP = ps("TP", [128, 2, 512], BF16)

        B = {}

        def nb(name, chan=False):
            B[name] = S.cbuf(name) if chan else Buf(name)
            return B[name]

        bx2 = [[nb(f"x{i}_{g}", chan=True) for g in range(4)] for i in range(2)]
        b_acc_all = [nb(f"acc_all{j}") for j in range(8)]
        b_xn_tok = [nb(f"xn_tok{g}") for g in range(4)]
        b_xnT = [nb(f"xnT{j}") for j in range(8)]
        b_rnn = [nb(f"rnn{j}") for j in range(8)]
        b_KT = nb("KT", chan=True)
        b_V = nb("V_all", chan=True)
        b_ptown = [[nb(f"ptown{h}{i}") for i in range(2)] for h in range(2)]
        b_QT = [nb(f"QT{j}") for j in range(8)]
        b_mix = [nb(f"mix{j}") for j in range(8)]
        b_CS = nb("CS", chan=True)
        b_tmpf = [nb(f"tf{i}", chan=True) for i in range(NTMP)]
        b_tmpb = [nb(f"tb{i}") for i in range(NTB)]
        b_tmpg = [nb(f"tg{i}") for i in range(NTG)]
        b_yst = [nb(f"yst{i}", chan=True) for i in range(1)]
        b_gates = nb("gates", chan=True)
        b_wv = nb("wv", chan=True)
        b_ring = [nb(f"ring{i}", chan=True) for i in range(NR)]
        b_prm = nb("prm", chan=True)
        b_gF = nb("gF", chan=True)
        b_cst = nb("cst")
        b_ident = nb("ident")
        b_stat = nb("stat")
        b_hst = {"p": [nb(f"hst_p{j}", chan=(j == 0)) for j in range(8)],
                 "s": [nb(f"hst_s{j}", chan=(j == 0)) for j in range(8)]}
        b_convst = {"p": nb("convst_p", chan=True), "s": nb("convst_s", chan=True)}
        b_ffnst = {"p": nb("ffnst_p", chan=True), "s": nb("ffnst_s", chan=True)}
        b_PG = [nb(f"PG{i}") for i in range(3)]
        b_PH = [nb(f"PH{i}") for i in range(4)]
        b_TP = nb("TP")
        b_out = nb("dram_out")
        b_in = nb("dram_in")
        b_hT_all = b_QT + b_mix

        hst = {"p": hst_p, "s": hst_s}
        convst = {"p": convst_p, "s": convst_s}
        ffnst = {"p": ffnst_p, "s": ffnst_s}

        cnt = {"tf": 0, "tb": 0, "pg": 0, "yst": 0, "tg": 0}

        def tf():
            i = cnt["tf"] % NTMP
            cnt["tf"] += 1
            b_tmpf[i] = b_tmpf[i].renew()
            return tmpf[:, i, :], b_tmpf[i]

        def tg():
            i = cnt["tg"] % NTG
            cnt["tg"] += 1
            b_tmpg[i] = b_tmpg[i].renew()
            return tmpg[:, i, :], b_tmpg[i]

        def tb():
            i = cnt["tb"] % NTB
            cnt["tb"] += 1
            b_tmpb[i] = b_tmpb[i].renew()
            return tmpb[:, i, :], b_tmpb[i]

        def pg():
            i = cnt["pg"] % 3
            cnt["pg"] += 1
            b_PG[i] = b_PG[i].renew()
            return PG[:, i, :], b_PG[i]

        S.dma("sp", prm[:], prm_d, reads=[b_in], writes=[b_prm])
        S.dma("sp", gF[:], gF_d, reads=[b_in], writes=[b_gF])
        cast_buf = {}
        pending_casts = []

        def add_cast(keys, dst, src):
            b = S.cbuf("cast_" + str(keys[0]))
            for k in keys:
                cast_buf[k] = (b, False)
            pending_casts.append((keys, dst, src, b))

        add_cast([("i", u) for u in range(0, 4)], wi_s[0:4], w_in_d[0:4])
        for u0 in range(4, 28, 2):
            add_cast([("i", u) for u in range(u0, u0 + 2)], wi_s[u0:u0 + 2], w_in_d[u0:u0 + 2])
        wo_v = wo_s.rearrange("u p c -> p u c")
        wo_i = w_out_d.rearrange("p (u c) -> p u c", u=4)
        for u0 in range(0, 4, 2):
            add_cast([("o", u) for u in range(u0, u0 + 2)], wo_v[:, u0:u0 + 2, :], wo_i[:, u0:u0 + 2, :])
        for u0 in range(0, NF, 2):
            add_cast([("u", u) for u in range(u0, u0 + 2)], wu_s[u0:u0 + 2], w_up_d[u0:u0 + 2])
        for u0 in range(0, 16, 2):
            add_cast([("d", u) for u in range(u0, u0 + 2)], wd_s[u0:u0 + 2], w_dn_d[u0:u0 + 2])

        def cast_some(n):
            for _ in range(n):
                if not pending_casts:
                    return
                keys, dst, src, b = pending_casts.pop(0)
                S.dma("pool", dst, src, reads=[b_in], writes=[b])
                for k in keys:
                    cast_buf[k] = (b, True)

        def cast_until(key):
            while not cast_buf[key][1]:
                cast_some(1)

        S.op("pool", lambda e: e.memset(ident_f[:], 0.0), writes=[b_ident])
        S.op("pool", lambda e: e.affine_select(out=ident_f[:], in_=ident_f[:], pattern=[[-1, 128]],
                                               compare_op=ALU.not_equal, fill=1.0, base=0,
                                               channel_multiplier=1), reads=[b_ident], writes=[b_ident])
        S.op("pool", lambda e: e.tensor_copy(out=ident_b[:], in_=ident_f[:]), reads=[b_ident], writes=[b_ident])
        S.op("pool", lambda e: e.memset(ones_b[:], 1.0), writes=[b_ident])
        S.op("pool", lambda e: e.memset(KT0[:], 0.0), writes=[b_KT])
        S.op("pool", lambda e: e.memset(KT1[:], 0.0), writes=[b_KT])
        S.op("pool", lambda e: e.memset(V_all[:], 0.0), writes=[b_V])
        S.op("pool", lambda e: e.memset(V2[:], 0.0), writes=[b_V])
        S.op("pool", lambda e: e.memset(ptown[:], 0.0), writes=[b_ptown[0][0], b_ptown[0][1], b_ptown[1][0], b_ptown[1][1]])
        S.op("pool", lambda e: e.memset(hst_p[:], 0.0), writes=b_hst["p"])
        S.op("pool", lambda e: e.memset(convst_p[:], 0.0), writes=[b_convst["p"]])
        S.op("pool", lambda e: e.memset(ffnst_p[:], 0.0), writes=[b_ffnst["p"]])
        C_HCL, C_ES, C_HBA, C_HBI, C_NEGH, C_SCR, C_CL = 0, 8, 16, 24, 32, 40, 56
        S.op("pool", lambda e: e.memset(cst[:, C_NEGH:C_NEGH + 8], -0.5), writes=[b_cst])
        ev = cst[:, C_SCR:C_SCR + 8]
        e2 = cst[:, C_SCR + 8:C_SCR + 16]
        S.op("act", lambda e: e.activation(out=ev, in_=prm[:, P_LAM:P_LAM + 8], func=AF.Exp, scale=-1.0),
             reads=[b_prm], writes=[b_cst])
        S.op("dve", lambda e: e.tensor_scalar(out=e2, in0=ev, scalar1=-0.25, scalar2=1.0 / 3.0,
                                              op0=ALU.mult, op1=ALU.add), reads=[b_cst], writes=[b_cst])
        S.op("dve", lambda e: e.tensor_tensor(out=e2, in0=e2, in1=ev, op=ALU.mult), reads=[b_cst], writes=[b_cst])
        S.op("dve", lambda e: e.tensor_scalar(out=e2, in0=e2, scalar1=-1.0, scalar2=0.5,
                                              op0=ALU.mult, op1=ALU.add), reads=[b_cst], writes=[b_cst])
        S.op("dve", lambda e: e.tensor_tensor(out=e2, in0=e2, in1=ev, op=ALU.mult), reads=[b_cst], writes=[b_cst])
        S.op("dve", lambda e: e.tensor_scalar(out=e2, in0=e2, scalar1=-1.0, scalar2=1.0,
                                              op0=ALU.mult, op1=ALU.add), reads=[b_cst], writes=[b_cst])
        S.op("dve", lambda e: e.tensor_tensor(out=e2, in0=e2, in1=ev, op=ALU.mult), reads=[b_cst], writes=[b_cst])
        S.op("dve", lambda e: e.tensor_scalar(out=cst[:, C_HCL:C_HCL + 8], in0=e2, scalar1=-4.0, scalar2=None,
                                              op0=ALU.mult), reads=[b_cst], writes=[b_cst])
        S.op("dve", lambda e: e.tensor_scalar(out=cst[:, C_CL:C_CL + 8], in0=e2, scalar1=-8.0, scalar2=None,
                                              op0=ALU.mult), reads=[b_cst], writes=[b_cst])
        S.op("act", lambda e: e.activation(out=cst[:, C_ES:C_ES + 8], in_=prm[:, P_SINK:P_SINK + 8], func=AF.Exp),
             reads=[b_prm], writes=[b_cst])
        S.op("dve", lambda e: e.tensor_scalar(out=cst[:, C_HBA:C_HBA + 8], in0=prm[:, P_BA:P_BA + 8], scalar1=0.5,
                                              scalar2=None, op0=ALU.mult), reads=[b_prm, b_cst], writes=[b_cst])
        S.op("dve", lambda e: e.tensor_scalar(out=cst[:, C_HBI:C_HBI + 8], in0=prm[:, P_BI:P_BI + 8], scalar1=0.5,
                                              scalar2=None, op0=ALU.mult), reads=[b_prm, b_cst], writes=[b_cst])

        cast_some(1)
        S.dma("pool", gates_sb[:].rearrange("p j a c -> p (j a c)"), gates_d, reads=[b_in], writes=[b_gates])
        S.dma("pool", w_v_sb[:].rearrange("p k c -> p (k c)"), w_v_d, reads=[b_in], writes=[b_wv])
        S.dma("sp", hst_s[:].rearrange("p j s -> p (j s)"), s_h_d, reads=[b_in], writes=b_hst["s"])
        S.dma("sp", convst_s[:].rearrange("p j s r -> p (j s r)"), s_conv_d, reads=[b_in], writes=[b_convst["s"]])
        S.dma("sp", ffnst_s[:].rearrange("p f s r -> p (f s r)"), s_ffn_d, reads=[b_in], writes=[b_ffnst["s"]])
        ch_misc = S.new_chan()
        if not DBG.get("no_misc"):
          S.dma("sp", okold_s, s_k_d[:, 64:128, :], reads=[b_in], writes=[], chan=ch_misc, final=True)
          S.dma("sp", ovold_s, s_v_d[:, 64:128, :], reads=[b_in], writes=[], chan=ch_misc, final=True)

        rs = {"next_dma": 0, "next_use": 0}

        def unit_src(u):
            kind, i = u
            cast_until(u)
            bsrc = cast_buf[u][0]
            if kind == "i":
                return wi_s[i], 2048, bsrc
            if kind == "u":
                return wu_s[i], 2048, bsrc
            if kind == "o":
                return wo_s[i], 2048, bsrc
            return wd_s[i], 1408, bsrc

        def ring_issue():
            n = rs["next_dma"]
            if n >= len(units):
                return
            src, w, bsrc = unit_src(units[n])
            slot = n % NR
            S.dma("sp", ring[:, slot, 0:w], src, reads=[bsrc], writes=[b_ring[slot]])
            rs["next_dma"] += 1

        def ring_take(expect):
            n = rs["next_use"]
            assert units[n] == expect, (units[n], expect)
            assert rs["next_dma"] > n
            rs["next_use"] += 1
            slot = n % NR
            return ring[:, slot, :], b_ring[slot]

        for _ in range(NR):
            if DBG.get("no_ring"):
                break
            ring_issue()

        def norm_T(x_sb, bx, T, gcol):
            NG = T // 128
            for g in range(NG):
                S.op("act", lambda e, g=g: e.activation(out=xn_tok[:, g, :], in_=x_sb[:, g, :], func=AF.Square,
                                                        accum_out=stat[:, g:g + 1]),
                     reads=[bx[g]], writes=[b_xn_tok[g], b_stat])
            S.op("dve", lambda e: e.tensor_scalar(out=stat[:, 4:4 + NG], in0=stat[:, 0:NG], scalar1=1.0 / D,
                                                  scalar2=EPS, op0=ALU.mult, op1=ALU.add),
                 reads=[b_stat], writes=[b_stat])
            S.op("act", lambda e: e.activation(out=stat[:, 12:12 + NG], in_=stat[:, 4:4 + NG], func=AF.Sqrt),
                 reads=[b_stat], writes=[b_stat])
            S.op("dve", lambda e: e.reciprocal(out=stat[:, 8:8 + NG], in_=stat[:, 12:12 + NG]),
                 reads=[b_stat], writes=[b_stat])
            for g in range(NG):
                S.op("dve", lambda e, g=g: e.tensor_scalar(out=xn_tok[:, g, :], in0=x_sb[:, g, :],
                                                           scalar1=stat[:, 8 + g:9 + g], scalar2=None, op0=ALU.mult),
                     reads=[bx[g], b_stat], writes=[b_xn_tok[g]])
            for r in range(4):
                def tr(e, r=r):
                    last = None
                    for jj in range(2):
                        j = 2 * r + jj
                        for g in range(NG):
                            last = e.transpose(out=TP[:, jj, g * 128:(g + 1) * 128],
                                               in_=xn_tok[:, g, j * 128:(j + 1) * 128], identity=ident_b[:])
                    return last
                S.op("pe", tr, reads=b_xn_tok[:NG] + [b_ident], writes=[b_TP])
                for jj in range(2):
                    j = 2 * r + jj
                    S.op("act", lambda e, j=j, jj=jj: e.activation(out=xnT[:, j, 0:T], in_=TP[:, jj, 0:T], func=AF.Copy,
                                                                   scale=prm[:, gcol + j:gcol + j + 1]),
                         reads=[b_TP, b_prm], writes=[b_xnT[j]])

        def mm_fm(out_ap, wview, f, c0, c1, reads, writes, rhs=None):
            def fn(e):
                last = None
                for k in range(8):
                    last = e.matmul(out_ap, lhsT=wview[:, f, k, :], rhs=xnT[:, k, c0:c1], start=(k == 0), stop=(k == 7))
                return last
            S.op("pe", fn, reads=reads + b_xnT, writes=writes)

        def unit_view(slot_ap):
            return slot_ap.rearrange("p (f k c) -> p f k c", f=2, k=8)

        def lru_a(run, j, xr_ps, b_xr_ps, T, nseg, L):
            W = 3 + L
            xr, b_xr = tf()
            xr3 = xr[:, 0:nseg * W].rearrange("p (s c) -> p s c", c=W)
            cs_ = convst[run]
            S.op("act", lambda e: e.activation(out=xr3[:, :, 3:3 + L], in_=xr_ps.rearrange("p (s l) -> p s l", l=L),
                                               func=AF.Copy), reads=[b_xr_ps], writes=[b_xr])
            S.op("pool", lambda e: e.tensor_copy(out=xr3[:, :, 0:3], in_=cs_[:, j, :, :]),
                 reads=[b_convst[run]], writes=[b_xr])
            acc, b_acc = acc_all[:, j, :], b_acc_all[j]
            acc3 = acc[:, 0:T].rearrange("p (s l) -> p s l", l=L)
            cw = lambda i: prm[:, P_CW + 4 * j + i:P_CW + 4 * j + i + 1]
            S.op("act", lambda e: e.activation(out=acc3, in_=xr_ps.rearrange("p (s l) -> p s l", l=L), func=AF.Identity,
                                               scale=cw(3), bias=prm[:, P_CB + j:P_CB + j + 1]),
                 reads=[b_xr_ps, b_prm], writes=[b_acc])
            for i in range(0, 3):
                S.op("dve", lambda e, i=i: e.scalar_tensor_tensor(out=acc3, in0=xr3[:, :, i:i + L], scalar=cw(i), in1=acc3,
                                                                  op0=ALU.mult, op1=ALU.add),
                     reads=[b_xr, b_prm, b_acc], writes=[b_acc])
            S.op("pool", lambda e: e.tensor_copy(out=cs_[:, j, :, :], in_=xr3[:, :, L:L + 3]),
                 reads=[b_xr], writes=[b_convst[run]])

        def lru_stages(run, js, T, nseg, L):
            st_ = {}
            aux_eng = "pool" if run_is_pre[0] else "dve"
            hs = hst[run]

            def s0():
                for j in js:
                    acc, b_acc = acc_all[:, j, :], b_acc_all[j]
                    xcb, b_xcb = tb()
                    S.op(aux_eng, lambda e, xcb=xcb, acc=acc: e.tensor_copy(out=xcb[:, 0:T], in_=acc[:, 0:T]),
                         reads=[b_acc], writes=[b_xcb])
                    st_[j] = dict(acc=acc, b_acc=b_acc, xcb=xcb, b_xcb=b_xcb)

            def s1():
                for j in js:
                    d = st_[j]
                    xcb, b_xcb, acc, b_acc = d["xcb"], d["b_xcb"], d["acc"], d["b_acc"]
                    ga, b_ga = pg()
                    S.op("pe", lambda e, ga=ga, xcb=xcb, j=j: e.matmul(ga[:, 0:T], lhsT=gates_sb[:, j, 0, :], rhs=xcb[:, 0:T],
                                                                       start=True, stop=True),
                         reads=[b_gates, b_xcb], writes=[b_ga])
                    gi, b_gi = pg()
                    S.op("pe", lambda e, gi=gi, xcb=xcb, j=j: e.matmul(gi[:, 0:T], lhsT=gates_sb[:, j, 1, :], rhs=xcb[:, 0:T],
                                                                       start=True, stop=True),
                         reads=[b_gates, b_xcb], writes=[b_gi])
                    rp, b_rp = tf()
                    ip, b_ip = tg()
                    S.op("act", lambda e, rp=rp, ga=ga, j=j: e.activation(out=rp[:, 0:T], in_=ga[:, 0:T], func=AF.Tanh,
                                                                          scale=0.5, bias=cst[:, C_HBA + j:C_HBA + j + 1]),
                         reads=[b_ga, b_cst], writes=[b_rp])
                    S.op("act", lambda e, ip=ip, gi=gi, j=j: e.activation(out=ip[:, 0:T], in_=gi[:, 0:T], func=AF.Tanh,
                                                                          scale=0.5, bias=cst[:, C_HBI + j:C_HBI + j + 1]),
                         reads=[b_gi, b_cst], writes=[b_ip])
                    S.op("dve", lambda e, ip=ip, acc=acc: e.scalar_tensor_tensor(out=acc[:, 0:T], in0=ip[:, 0:T], scalar=1.0,
                                                                                 in1=acc[:, 0:T], op0=ALU.add, op1=ALU.mult),
                         reads=[b_ip, b_acc], writes=[b_acc])
                    d.update(rp=rp, b_rp=b_rp)

            def s2():
                for j in js:
                    d = st_[j]
                    rp, b_rp = d["rp"], d["b_rp"]
                    S.op("act", lambda e, rp=rp, j=j: e.activation(out=rp[:, 0:T], in_=rp[:, 0:T], func=AF.Exp,
                                                                   scale=cst[:, C_HCL + j:C_HCL + j + 1],
                                                                   bias=cst[:, C_HCL + j:C_HCL + j + 1]),
                         reads=[b_rp, b_cst], writes=[b_rp])

            def s3():
                for j in js:
                    d = st_[j]
                    rp, b_rp = d["rp"], d["b_rp"]
                    S.op(aux_eng, lambda e, rp=rp, j=j: e.tensor_tensor(out=rnn[:, j, 0:T], in0=rp[:, 0:T], in1=rp[:, 0:T],
                                                                        op=ALU.mult),
                         reads=[b_rp], writes=[b_rnn[j]])

            def s4():
                for j in js:
                    S.op("act", lambda e, j=j: e.activation(out=rnn[:, j, 0:T], in_=rnn[:, j, 0:T], func=AF.Sqrt,
                                                            scale=-1.0, bias=1.0),
                         reads=[b_rnn[j]], writes=[b_rnn[j]])

            def s5():
                for j in js:
                    d = st_[j]
                    acc, b_acc = d["acc"], d["b_acc"]
                    S.op("dve", lambda e, acc=acc, j=j: e.scalar_tensor_tensor(out=acc[:, 0:T], in0=acc[:, 0:T], scalar=0.5,
                                                                               in1=rnn[:, j, 0:T], op0=ALU.mult, op1=ALU.mult),
                         reads=[b_acc, b_rnn[j]], writes=[b_acc])

            def s6():
                for j in js:
                    d = st_[j]
                    rp, b_rp, acc, b_acc = d["rp"], d["b_rp"], d["acc"], d["b_acc"]
                    for s in range(nseg):
                        S.op("dve", lambda e, s=s, j=j, rp=rp, acc=acc: e.tensor_tensor_scan(
                            out=rnn[:, j, s * L:(s + 1) * L], data0=rp[:, s * L:(s + 1) * L],
                            data1=acc[:, s * L:(s + 1) * L], initial=hs[:, j, s:s + 1], op0=ALU.mult, op1=ALU.add),
                             reads=[b_rp, b_acc, b_hst[run][j]], writes=[b_rnn[j]])

            def s7():
                for j in js:
                    S.op("pool", lambda e, j=j: e.tensor_copy(
                        out=hs[:, j, :], in_=rnn[:, j, 0:T].rearrange("p (s l) -> p s l", l=L)[:, :, L - 1]),
                         reads=[b_rnn[j]], writes=[b_hst[run][j]])

            return [s0, s1, s2, s3, s4, s5, s6, s7]

        def lru_b_chains(run, T, nseg, L, fillers=(), offset=1):
            g1 = lru_stages(run, [0, 1, 2, 3], T, nseg, L)
            g2 = lru_stages(run, [4, 5, 6, 7], T, nseg, L)
            fillers = list(fillers)
            nst = len(g1)
            for step in range(nst + offset):
                if step < nst:
                    g1[step]()
                if 0 <= step - offset < nst:
                    g2[step - offset]()
                if fillers and fillers[0][0] <= step:
                    fillers.pop(0)[1]()
            while fillers:
                fillers.pop(0)[1]()

        run_is_pre = [False]

        def rope_pair(p_ps, b_p, r_ps, b_r, T, b_dst, out_views, f32_dma=None, alloc=None):
            alloc = alloc or tf
            t1, b_t1 = alloc()
            t2, b_t2 = alloc()
            S.op("dve", lambda e: e.tensor_tensor(out=t1[:, 0:T], in0=p_ps, in1=CS[:, 0, 0:T], op=ALU.mult),
                 reads=[b_p, b_CS], writes=[b_t1])
            S.op("dve", lambda e: e.tensor_tensor(out=t2[:, 0:T], in0=r_ps, in1=CS[:, 1, 0:T], op=ALU.mult),
                 reads=[b_r, b_CS], writes=[b_t2])
            if f32_dma is not None:
                S.op("pool", lambda e: e.tensor_tensor(out=t1[:, 0:T], in0=t1[:, 0:T], in1=t2[:, 0:T], op=ALU.add),
                     reads=[b_t1, b_t2], writes=[b_t1])
                for dst, (p0, p1), (c0, c1) in out_views:
                    S.op("pool", lambda e, dst=dst, p0=p0, p1=p1, c0=c0, c1=c1: e.tensor_copy(out=dst, in_=t1[p0:p1, c0:c1]),
                         reads=[b_t1], writes=[b_dst])
                dram_ap, (c0, c1) = f32_dma
                S.dma("sp", dram_ap, t1[:, c0:c1], reads=[b_t1], writes=[], final=True)
            else:
                for dst, (p0, p1), (c0, c1) in out_views:
                    S.op("pool", lambda e, dst=dst, p0=p0, p1=p1, c0=c0, c1=c1: e.tensor_tensor(
                        out=dst, in0=t1[p0:p1, c0:c1], in1=t2[p0:p1, c0:c1], op=ALU.add),
                         reads=[b_t1, b_t2], writes=[b_dst])

        def x_of(ti):
            return x_sb2[:, ti % 2], bx2[ti % 2]

        def load_x(ti):
            if ti >= len(tiles) or ti >= DBG.get("ntiles", 10 ** 9):
                return
            t = tiles[ti]
            if t.get("x_loaded"):
                return
            t["x_loaded"] = True
            xs, bxs = x_of(ti)
            src = xsrc[t["src"]]
            for g in range(t["T"] // 128):
                r0 = t["row0"] + g * 128
                S.dma("sp", xs[:, g, :], src[r0:r0 + 128, :], reads=[b_in], writes=[bxs[g]])

        def load_cs(t):
            if t.get("cs_done"):
                return
            t["cs_done"] = True
            if not (t["mode"] != "pre" or t.get("kv_last")):
                return
            off, T_, L_ = t["rope_off"], t["T"], t["L"]
            if t.get("sample"):
                for s in range(t["nseg"]):
                    S.dma("sp", CS[:, 0, s * L_:(s + 1) * L_], ropeC[:, off:off + L_], reads=[b_in], writes=[b_CS])
                    S.dma("sp", CS[:, 1, s * L_:(s + 1) * L_], ropeS[:, off:off + L_], reads=[b_in], writes=[b_CS])
            else:
                w = T_ if t["mode"] != "pre" else 128
                S.dma("sp", CS[:, 0, 0:w], ropeC[:, off:off + w], reads=[b_in], writes=[b_CS])
                S.dma("sp", CS[:, 1, 0:w], ropeS[:, off:off + w], reads=[b_in], writes=[b_CS])

        def stage_a(ti):
            xs, bxs = x_of(ti)
            norm_T(xs, bxs, tiles[ti]["T"], P_G1)

        def emit_tile(ti, t):
            mode, T, nseg, L, run = t["mode"], t["T"], t["nseg"], t["L"], t["run"]
            run_is_pre[0] = (mode == "pre")
            NG = T // 128
            ncs = L // 64
            nch = T // 64
            full = mode != "pre"
            sample = t.get("sample", False)
            x_sb, bx = x_of(ti)
            nxt = tiles[ti + 1] if (ti + 1 < len(tiles) and ti + 1 < DBG.get("ntiles", 10 ** 9)) else None

            load_x(ti + 1)
            if mode == "pre":
                load_x(ti + 2)
            if not t.get("cs_done"):
                load_cs(t)
            if sample:
                skv = s_kT_d.rearrange("p (k s c) -> p k s c", k=NKV, s=nseg)
                for z in range(2):
                    ktv = KTz[z][:, :, 0:nseg * 192].rearrange("p k (s c) -> p k s c", c=192)
                    for kv in range(NKV):
                        S.dma("pool", ktv[z * 64:(z + 1) * 64, kv, :, 0:128], skv[z * 64:(z + 1) * 64, kv, :, :],
                              reads=[b_in], writes=[b_KT])
                S.dma("pool", V_all[:, 0:nseg, :], s_v_d.rearrange("s p c -> p s c"), reads=[b_in], writes=[b_V])

            def kt_col(seg, slot):
                return (seg * (2 + ncs) + slot) * 64

            hslot = (lambda s: s) if sample else (lambda s: 0)
            own_base = nseg if sample else 1
            k_out = t.get("last", False)

            def k_unit(tt, kv):
                T_, nseg_, L_ = tt["T"], tt["nseg"], tt["L"]
                ncs_ = L_ // 64
                sample_ = tt.get("sample", False)
                slot, b_slot = ring_take(("i", 4 + kv))
                wv = unit_view(slot)
                kb, b_kb = pg()
                mm_fm(kb[:, 0:T_], wv, 0, 0, T_, [b_slot], [b_kb])
                rb, b_rb = pg()
                mm_fm(rb[:, 0:T_], wv, 1, 0, T_, [b_slot], [b_rb])
                ring_issue()
                views = []
                for s in range(nseg_):
                    c0 = (s * (2 + ncs_) + 2) * 64
                    for z in range(2):
                        views.append((KTz[z][z * 64:(z + 1) * 64, kv, c0:c0 + L_], (z * 64, (z + 1) * 64),
                                      (s * L_, (s + 1) * L_)))
                f32o = None
                if tt.get("last", False):
                    if sample_:
                        f32o = (okT_s[:, kv * T_:(kv + 1) * T_], (0, T_))
                    else:
                        f32o = (okT_p[:, kv * 128:(kv + 1) * 128], (T_ - 128, T_))
                rope_pair(kb[:, 0:T_], b_kb, rb[:, 0:T_], b_rb, T_, b_KT, views, f32_dma=f32o)

            if mode != "pre" and not t.get("kx_done"):
                for kv in range(NKV):
                    k_unit(t, kv)

            def xr_units(tt, us):
                T_, nseg_, L_, run_ = tt["T"], tt["nseg"], tt["L"], tt["run"]
                for u in us:
                    slot, b_slot = ring_take(("i", u))
                    wv = unit_view(slot)
                    for f in range(2):
                        j = 2 * u + f
                        bank, b_bank = pg()
                        mm_fm(bank[:, 0:T_], wv, f, 0, T_, [b_slot], [b_bank])
                        lru_a(run_, j, bank[:, 0:T_], b_bank, T_, nseg_, L_)
                    ring_issue()

            if not t.get("xr_done") and not t.get("kx_done"):
                xr_units(t, range(4))

            if mode == "pre":
                if t.get("kv_last"):
                    for kv in range(NKV):
                        slot, b_slot = ring_take(("i", 4 + kv))
                        wv = unit_view(slot)
                        kb, b_kb = pg()
                        mm_fm(kb[:, 0:128], wv, 0, T - 128, T, [b_slot], [b_kb])
                        rb, b_rb = pg()
                        mm_fm(rb[:, 0:128], wv, 1, T - 128, T, [b_slot], [b_rb])
                        ring_issue()
                        rope_pair(kb[:, 0:128], b_kb, rb[:, 0:128], b_rb, 128, b_KT,
                                  [(KTz[z][z * 64:(z + 1) * 64, kv, 0:128], (z * 64, (z + 1) * 64), (0, 128))
                                   for z in range(2)])
                    vb, b_vb = pg()

                    def vfn(e):
                        last = None
                        for k in range(8):
                            last = e.matmul(vb[:, 0:256], lhsT=xnT[:, k, T - 128:T], rhs=w_v_sb[:, k, :],
                                            start=(k == 0), stop=(k == 7))
                        return last
                    S.op("pe", vfn, reads=b_xnT + [b_wv], writes=[b_vb])
                    S.op("act", lambda e: e.activation(out=V_all[:, 0, :], in_=vb[:, 0:256], func=AF.Copy),
                         reads=[b_vb], writes=[b_V])
                if nxt is not None and nxt["mode"] == "pre":
                    stage_a(ti + 1)
                    lru_b_chains(run, T, nseg, L, fillers=[
                        (1, lambda: cast_some(1)), (3, lambda: cast_some(1)), (5, lambda: cast_some(1)),
                        (8, lambda: xr_units(nxt, [0, 1])), (9, lambda: cast_some(1)),
                        (9, lambda: xr_units(nxt, [2, 3])), (9, lambda: cast_some(1))])
                    nxt["xr_done"] = True
                else:
                    lru_b_chains(run, T, nseg, L)
                    if nxt is not None:
                        stage_a(ti + 1)
                    cast_some(5)
                return

            if DBG.get("stage", 99) <= 1 and mode != "pre":
                return
            if not sample:
                S.op("pool", lambda e: e.tensor_copy(out=V2[64:128, 1, :], in_=V_all[64:128, 0, :]),
                     reads=[b_V], writes=[b_V])

            if DBG.get("stage", 99) <= 2 and mode != "pre":
                return
            def v_units(tt):
                T_ = tt["T"]
                NG_ = T_ // 128
                sample_ = tt.get("sample", False)
                ob_ = tt["nseg"] if sample_ else 1
                ko_ = tt.get("last", False)
                for g in range(NG_):
                    vb, b_vb = pg()

                    def vfn(e, g=g, vb=vb):
                        last = None
                        for k in range(8):
                            last = e.matmul(vb[:, 0:256], lhsT=xnT[:, k, g * 128:(g + 1) * 128], rhs=w_v_sb[:, k, :],
                                            start=(k == 0), stop=(k == 7))
                        return last
                    S.op("pe", vfn, reads=b_xnT + [b_wv], writes=[b_vb])
                    S.op("act", lambda e, g=g, vb=vb: e.activation(out=V_all[:, ob_ + g, :], in_=vb[:, 0:256], func=AF.Copy),
                         reads=[b_vb], writes=[b_V])
                    if not sample_:
                        S.op("act", lambda e, g=g, vb=vb: e.activation(out=V2[0:64, ob_ + g, :], in_=vb[0:64, 0:256],
                                                                     func=AF.Copy), reads=[b_vb], writes=[b_V])
                        S.op("act", lambda e, g=g, vb=vb: e.activation(out=V2[64:128, ob_ + g + 1, :], in_=vb[64:128, 0:256],
                                                                     func=AF.Copy), reads=[b_vb], writes=[b_V])
                    if ko_ and (sample_ or g == NG_ - 1):
                        vo, b_vo = tf()
                        S.op("act", lambda e, vo=vo, vb=vb: e.activation(out=vo[:, 0:256], in_=vb[:, 0:256], func=AF.Copy),
                             reads=[b_vb], writes=[b_vo])
                        vdst = ov_s[g * 128:(g + 1) * 128, :] if sample_ else ov_p
                        S.dma("sp", vdst, vo[:, 0:256], reads=[b_vo], writes=[], final=True)

            if not t.get("vq_done"):
                v_units(t)
            if DBG.get("stage", 99) <= 3 and mode != "pre":
                return
            def q_unit(j, alloc=None):
                slot, b_slot = ring_take(("i", 8 + j))
                wv = unit_view(slot)
                qb, b_qb = pg()
                mm_fm(qb[:, 0:T], wv, 0, 0, T, [b_slot], [b_qb])
                rb, b_rb = pg()
                mm_fm(rb[:, 0:T], wv, 1, 0, T, [b_slot], [b_rb])
                ring_issue()
                rope_pair(qb[:, 0:T], b_qb, rb[:, 0:T], b_rb, T, b_QT[j], [(QT[:, j, 0:T], (0, 128), (0, T))], alloc=alloc)

            def yg_unit(u, alloc=None):
                alloc = alloc or tf
                slot, b_slot = ring_take(("i", 16 + u))
                wv = unit_view(slot)
                for f in range(2):
                    j = 2 * u + f
                    bank, b_bank = pg()
                    mm_fm(bank[:, 0:T], wv, f, 0, T, [b_slot], [b_bank])
                    gl, b_gl = alloc()
                    S.op("act", lambda e, bank=bank, gl=gl: e.activation(out=gl[:, 0:T], in_=bank[:, 0:T],
                                                                       func=AF.Gelu_apprx_tanh),
                         reads=[b_bank], writes=[b_gl])
                    S.op("dve", lambda e, j=j, gl=gl: e.tensor_tensor(out=rnn[:, j, 0:T], in0=rnn[:, j, 0:T],
                                                                      in1=gl[:, 0:T], op=ALU.mult),
                         reads=[b_rnn[j], b_gl], writes=[b_rnn[j]])
                ring_issue()

            def gr_unit(u, alloc=None):
                alloc = alloc or tf
                slot, b_slot = ring_take(("i", 20 + u))
                wv = unit_view(slot)
                for f in range(2):
                    j = 2 * u + f
                    bank, b_bank = pg()
                    mm_fm(bank[:, 0:T], wv, f, 0, T, [b_slot], [b_bank])
                    sg, b_sg = alloc()
                    S.op("act", lambda e, bank=bank, sg=sg: e.activation(out=sg[:, 0:T], in_=bank[:, 0:T], func=AF.Tanh,
                                                                       scale=0.5),
                         reads=[b_bank], writes=[b_sg])
                    S.op("dve", lambda e, j=j, sg=sg: e.scalar_tensor_tensor(out=rnn[:, j, 0:T], in0=sg[:, 0:T], scalar=1.0,
                                                                             in1=rnn[:, j, 0:T], op0=ALU.add, op1=ALU.mult),
                         reads=[b_rnn[j], b_sg], writes=[b_rnn[j]])
                ring_issue()

            if not t.get("vq_done"):
                for j in range(4):
                    q_unit(j)
            def _cs_next():
                if nxt is not None:
                    load_cs(nxt)
            lru_b_chains(run, T, nseg, L, fillers=[
                (1, lambda: q_unit(4, alloc=tg)), (3, lambda: q_unit(5, alloc=tg)),
                (5, lambda: q_unit(6, alloc=tg)), (6, lambda: q_unit(7, alloc=tg)),
                (7, _cs_next),
                (7, lambda: yg_unit(0, alloc=tg)), (8, lambda: yg_unit(1, alloc=tg)),
                (9, lambda: gr_unit(0, alloc=tg)), (10, lambda: gr_unit(1, alloc=tg))])
            yg_unit(2)
            yg_unit(3)
            gr_unit(2)
            gr_unit(3)

            if DBG.get("stage", 99) <= 5 and mode != "pre":
                return
            numf = PH[:, 0:2, :].rearrange("p b c -> p (b c)")
            denf = PH[:, 2:4, :].rearrange("p b c -> p (b c)")
            hb_first = t.get("first", False)
            for kv in range(NKV):
                ga_slot, ga_b_slot = ring_take(("i", 24 + kv))
                ga_wv = unit_view(ga_slot)
                sgs = []
                for a in range(2):
                    bank, b_bank = pg()
                    mm_fm(bank[:, 0:T], ga_wv, a, 0, T, [ga_b_slot], [b_bank])
                    sg, b_sg = tf()
                    S.op("act", lambda e, bank=bank, sg=sg: e.activation(out=sg[:, 0:T], in_=bank[:, 0:T], func=AF.Tanh,
                                                                       scale=0.5),
                         reads=[b_bank], writes=[b_sg])
                    sgs.append((sg, b_sg))
                ring_issue()

                def att_front(c, kv=kv):
                    s, lc = c // ncs, c % ncs
                    hp = lc % 2
                    hc = (s * ncs + lc) % 2
                    kcs = [(kt_col(s, lc), hp, 0), (kt_col(s, lc + 1), 1 - hp, 0), (kt_col(s, lc + 2), hc, 256)]
                    stb, b_stb = pg()

                    def sfn(e, kcs=kcs, stb=stb, c=c, kv=kv):
                        last = None
                        for (ktc, half, cb) in kcs:
                            for gp in range(2):
                                out = stb[half * 64:(half + 1) * 64, cb + gp * 128:cb + (gp + 1) * 128]
                                last = e.matmul(out, lhsT=KTz[gp][:, kv, ktc:ktc + 64],
                                                rhs=QT[:, 2 * kv:2 * kv + 2, c * 64:(c + 1) * 64],
                                                start=True, stop=True)
                        return last
                    S.op("pe", sfn, reads=[b_KT, b_QT[2 * kv], b_QT[2 * kv + 1]], writes=[b_stb])
                    pt, b_pt = tb()
                    if hb_first and lc < 2:
                        bias = prm[:, P_HB + lc:P_HB + lc + 1]
                        S.op("act", lambda e, stb=stb, pt=pt, bias=bias: e.activation(out=pt[:, 0:256], in_=stb[:, 0:256],
                                                                                     func=AF.Exp, scale=0.125, bias=bias),
                             reads=[b_stb, b_prm], writes=[b_pt])
                    else:
                        S.op("act", lambda e, stb=stb, pt=pt: e.activation(out=pt[:, 0:256], in_=stb[:, 0:256],
                                                                         func=AF.Exp, scale=0.125),
                             reads=[b_stb], writes=[b_pt])
                    oi = (c // 2) % 2
                    po = ptown[:, hc, oi, :]
                    b_po = b_ptown[hc][oi]
                    S.op("act", lambda e, stb=stb, po=po, hc=hc: e.activation(out=po[hc * 64:(hc + 1) * 64, :],
                                                                            in_=stb[hc * 64:(hc + 1) * 64, 256:512],
                                                                            func=AF.Exp, scale=0.125),
                         reads=[b_stb], writes=[b_po])
                    return dict(pt=pt, b_pt=b_pt, po=po, b_po=b_po, s=s, lc=lc, hp=hp, c=c)

                def att_back(inf, kv=kv):
                    pt, b_pt, po, b_po, s, lc, hp, c = (inf[k] for k in ('pt', 'b_pt', 'po', 'b_po', 's', 'lc', 'hp', 'c'))
                    if lc == 0:
                        vpair = V_all[:, hslot(s), :]
                    elif hp == 0:
                        vpair = V_all[:, own_base + (s * ncs + lc - 2) // 2, :]
                    else:
                        vpair = V2[:, own_base + (s * ncs + lc - 1) // 2, :]
                    vown = V_all[:, own_base + (s * ncs + lc) // 2, :]

                    def pvfn(e, pt=pt, po=po, vpair=vpair, vown=vown, c=c, kv=kv):
                        last = None
                        for tgt, is_den in ((numf, False), (denf, True)):
                            for gp in range(2):
                                out = tgt[gp * 64:(gp + 1) * 64, c * 128:(c + 1) * 128]
                                l0 = ones_b[:, :] if is_den else vpair[:, kv * 64:(kv + 1) * 64]
                                l1 = ones_b[:, :] if is_den else vown[:, kv * 64:(kv + 1) * 64]
                                e.matmul(out, lhsT=l0, rhs=pt[:, gp * 128:(gp + 1) * 128], start=True, stop=False)
                                last = e.matmul(out, lhsT=l1, rhs=po[:, gp * 128:(gp + 1) * 128], start=False, stop=True)
                        return last
                    S.op("pe", pvfn, reads=[b_pt, b_po, b_V, b_ident], writes=b_PH)

                pend = []
                for c in range(nch + 2):
                    if c < nch:
                        pend.append(att_front(c))
                    if c >= 2 or c >= nch:
                        if pend:
                            att_back(pend.pop(0))
                while pend:
                    att_back(pend.pop(0))
                if DBG.get("att", 9) <= 3:
                    continue
                for a in range(2):
                    j = 2 * kv + a
                    nv = numf[:, 0:nch * 128].rearrange("p (c a q) -> p c a q", a=2, q=64)[:, :, a, :]
                    dv = denf[:, 0:nch * 128].rearrange("p (c a q) -> p c a q", a=2, q=64)[:, :, a, :]
                    rec, b_rec = tf()
                    rec3 = rec[:, 0:T].rearrange("p (c q) -> p c q", q=64)
                    S.op("dve", lambda e, dv=dv, rec3=rec3, j=j: e.tensor_scalar(out=rec3, in0=dv,
                                                                                  scalar1=cst[:, C_ES + j:C_ES + j + 1],
                                                                                  scalar2=2.0, op0=ALU.add, op1=ALU.mult),
                         reads=b_PH + [b_cst], writes=[b_rec])
                    S.op("dve", lambda e, rec=rec: e.reciprocal(out=rec[:, 0:T], in_=rec[:, 0:T]),
                         reads=[b_rec], writes=[b_rec])
                    S.op("dve", lambda e, nv=nv, rec3=rec3: e.tensor_tensor(out=rec3, in0=nv, in1=rec3, op=ALU.mult),
                         reads=b_PH + [b_rec], writes=[b_rec])
                    sg, b_sg = sgs[a]
                    S.op("dve", lambda e, sg=sg, rec=rec: e.scalar_tensor_tensor(out=rec[:, 0:T], in0=sg[:, 0:T], scalar=1.0,
                                                                                 in1=rec[:, 0:T], op0=ALU.add, op1=ALU.mult),
                         reads=[b_sg, b_rec], writes=[b_rec])
                    S.op("dve", lambda e, j=j, rec=rec: e.scalar_tensor_tensor(out=mixedT[:, j, 0:T], in0=rnn[:, j, 0:T],
                                                                               scalar=0.5, in1=rec[:, 0:T],
                                                                               op0=ALU.mult, op1=ALU.add),
                         reads=[b_rnn[j], b_rec], writes=[b_mix[j]])

            if DBG.get("stage", 99) <= 6 and mode != "pre":
                return
            if not sample:
                for z in range(2):
                    S.op("pool", lambda e, z=z: e.tensor_copy(out=KTz[z][:, :, 0:128],
                                                              in_=KTz[z][:, :, nch * 64:nch * 64 + 128]),
                         reads=[b_KT], writes=[b_KT])
                S.op("pool", lambda e: e.tensor_copy(out=V_all[:, 0, :], in_=V_all[:, NG, :]),
                     reads=[b_V], writes=[b_V])
            if DBG.get("stage", 99) <= 7 and mode != "pre":
                return
            wo = [ring_take(("o", u)) for u in range(4)]
            wov = [w_[0].rearrange("p (k c) -> p k c", k=2) for w_ in wo]
            for g in range(NG):
                hp = (g % 2) * 2

                def wofn(e, g=g, hp=hp):
                    last = None
                    for hh in range(2):
                        for k in range(8):
                            last = e.matmul(PH[:, hp + hh, :], lhsT=mixedT[:, k, g * 128:(g + 1) * 128],
                                            rhs=wov[k // 2][:, k % 2, hh * 512:(hh + 1) * 512], start=(k == 0), stop=(k == 7))
                    return last
                S.op("pe", wofn, reads=b_mix + [w_[1] for w_ in wo], writes=[b_PH[hp], b_PH[hp + 1]])
                for hh in range(2):
                    S.op("dve", lambda e, g=g, hp=hp, hh=hh: e.tensor_tensor(out=x_sb[:, g, hh * 512:(hh + 1) * 512],
                                                                             in0=x_sb[:, g, hh * 512:(hh + 1) * 512],
                                                                             in1=PH[:, hp + hh, :], op=ALU.add),
                         reads=[bx[g], b_PH[hp + hh]], writes=[bx[g]])
            if DBG.get("stage", 99) <= 8 and mode != "pre":
                return
            for _ in range(4):
                ring_issue()
            norm_T(x_sb, bx, T, P_G2)
            if DBG.get("stage", 99) <= 9 and mode != "pre":
                return
            fs = ffnst[run]

            def ffn_back(f, accs):
                (ag, b_ag), (av, b_av) = accs
                S.op("act", lambda e, ag=ag: e.activation(out=ag[:, 0:T], in_=ag[:, 0:T], func=AF.Silu),
                     reads=[b_ag], writes=[b_ag])
                S.op("dve", lambda e, ag=ag, av=av, f=f: e.tensor_tensor(out=hT[:, f, 0:T], in0=ag[:, 0:T], in1=av[:, 0:T],
                                                                         op=ALU.mult),
                     reads=[b_ag, b_av], writes=b_hT_all)

            ffn_prev = None
            for f in range(NF):
                slot, b_slot = ring_take(("u", f))
                wv = unit_view(slot)
                accs = []
                for tt in range(2):
                    fi = 2 * f + tt
                    bank, b_bank = pg()
                    mm_fm(bank[:, 0:T], wv, tt, 0, T, [b_slot], [b_bank])
                    if mode == "warm":
                        S.op("act", lambda e, bank=bank, fi=fi: e.activation(
                            out=fs[:, fi, :, :], in_=bank[:, 0:T].rearrange("p (s l) -> p s l", l=L)[:, :, L - 2:L],
                            func=AF.Copy), reads=[b_bank], writes=[b_ffnst[run]])
                        continue
                    W2 = 2 + L
                    up, b_up = tf()
                    up3 = up[:, 0:nseg * W2].rearrange("p (s c) -> p s c", c=W2)
                    S.op("act", lambda e, bank=bank, up3=up3: e.activation(
                        out=up3[:, :, 2:2 + L], in_=bank[:, 0:T].rearrange("p (s l) -> p s l", l=L), func=AF.Copy),
                         reads=[b_bank], writes=[b_up])
                    S.op("pool", lambda e, up3=up3, fi=fi: e.tensor_copy(out=up3[:, :, 0:2], in_=fs[:, fi, :, :]),
                         reads=[b_ffnst[run]], writes=[b_up])
                    acc, b_acc = tf()
                    acc3 = acc[:, 0:T].rearrange("p (s l) -> p s l", l=L)
                    fw = lambda i, fi=fi: prm[:, P_FW + 3 * fi + i:P_FW + 3 * fi + i + 1]
                    S.op("act", lambda e, acc3=acc3, bank=bank, fi=fi, fw=fw: e.activation(
                        out=acc3, in_=bank[:, 0:T].rearrange("p (s l) -> p s l", l=L), func=AF.Identity,
                        scale=fw(2), bias=prm[:, P_FB + fi:P_FB + fi + 1]), reads=[b_bank, b_prm], writes=[b_acc])
                    for i in (0, 1):
                        S.op("dve", lambda e, acc3=acc3, up3=up3, i=i, fw=fw: e.scalar_tensor_tensor(
                            out=acc3, in0=up3[:, :, i:i + L], scalar=fw(i), in1=acc3, op0=ALU.mult, op1=ALU.add),
                             reads=[b_up, b_prm, b_acc], writes=[b_acc])
                    S.op("pool", lambda e, up3=up3, fi=fi: e.tensor_copy(out=fs[:, fi, :, :], in_=up3[:, :, L:L + 2]),
                         reads=[b_up], writes=[b_ffnst[run]])
                    accs.append((acc, b_acc))
                ring_issue()
                if mode == "warm":
                    continue
                if ffn_prev is not None:
                    ffn_back(*ffn_prev)
                ffn_prev = (f, accs)
            if ffn_prev is not None:
                ffn_back(*ffn_prev)
            if nxt is not None:
                stage_a(ti + 1)
            if mode == "warm":
                return
            def wd_back(j, dl, b_dl):
                hb_ = j % 4
                tpf = PH[:, hb_, :].rearrange("p (g c) -> p g c", c=128)

                def tfn(e, dl=dl, tpf=tpf):
                    last = None
                    for g in range(NG):
                        last = e.transpose(out=tpf[:, g, :], in_=dl[:, g * 128:(g + 1) * 128], identity=ident_f[:])
                    return last
                S.op("pe", tfn, reads=[b_dl, b_ident], writes=[b_PH[hb_]])
                S.op("dve", lambda e, j=j, tpf=tpf: e.tensor_tensor(out=x_sb[:, 0:NG, j * 128:(j + 1) * 128],
                                                                   in0=x_sb[:, 0:NG, j * 128:(j + 1) * 128],
                                                                   in1=tpf[:, 0:NG, :], op=ALU.add),
                     reads=bx[:NG] + [b_PH[hb_]], writes=bx[:NG])

            wd_fill = []
            if nxt is not None and kx_early(tiles, ti + 1):
                wd_fill = [(lambda kv=kv: k_unit(nxt, kv)) for kv in range(NKV)] + \
                          [(lambda u=u: xr_units(nxt, [u])) for u in range(4)]
                nxt["kx_done"] = True
            wd_prev = None
            for j in range(8):
                bank, b_bank = pg()
                for hh in range(2):
                    slot, b_slot = ring_take(("d", 2 * j + hh))
                    wd = slot[:, 0:1408].rearrange("p (k c) -> p k c", c=128)

                    def dfn(e, hh=hh, wd=wd, bank=bank):
                        last = None
                        for kk in range(11):
                            k = hh * 11 + kk
                            last = e.matmul(bank[:, 0:T], lhsT=wd[:, kk, :], rhs=hT[:, k, 0:T],
                                            start=(k == 0), stop=(k == 21))
                        return last
                    S.op("pe", dfn, reads=[b_slot] + b_hT_all, writes=[b_bank])
                    ring_issue()
                dl, b_dl = tf()
                S.op("act", lambda e, bank=bank, dl=dl: e.activation(out=dl[:, 0:T], in_=bank[:, 0:T], func=AF.Copy),
                     reads=[b_bank], writes=[b_dl])
                if wd_prev is not None:
                    wd_back(*wd_prev)
                wd_prev = (j, dl, b_dl)
                if wd_fill:
                    wd_fill.pop(0)()
            wd_back(*wd_prev)
            if nxt is not None and kx_early(tiles, ti + 1):
                v_units(nxt)
                for j in range(4):
                    q_unit(j)
                nxt["vq_done"] = True
            yd = ydst[t["ydst"]]
            for g in range(NG):
                S.op("act", lambda e, g=g: e.activation(out=xn_tok[:, g, :], in_=x_sb[:, g, :], func=AF.Square,
                                                        accum_out=stat[:, g:g + 1]),
                     reads=[bx[g]], writes=[b_xn_tok[g], b_stat])
            S.op("dve", lambda e: e.tensor_scalar(out=stat[:, 4:4 + NG], in0=stat[:, 0:NG], scalar1=1.0 / D,
                                                  scalar2=EPS, op0=ALU.mult, op1=ALU.add),
                 reads=[b_stat], writes=[b_stat])
            S.op("act", lambda e: e.activation(out=stat[:, 12:12 + NG], in_=stat[:, 4:4 + NG], func=AF.Sqrt),
                 reads=[b_stat], writes=[b_stat])
            S.op("dve", lambda e: e.reciprocal(out=stat[:, 8:8 + NG], in_=stat[:, 12:12 + NG]),
                 reads=[b_stat], writes=[b_stat])
            for g in range(NG):
                yi = 0
                cnt["yst"] += 1
                S.op("dve", lambda e, g=g, yi=yi: e.scalar_tensor_tensor(out=ystage[:, yi, :], in0=x_sb[:, g, :],
                                                                         scalar=stat[:, 8 + g:9 + g], in1=gF[:],
                                                                         op0=ALU.mult, op1=ALU.mult),
                     reads=[bx[g], b_stat, b_gF], writes=[b_yst[yi]])
                r0 = t["row0"] + g * 128
                S.dma("sp", yd[r0:r0 + 128, :], ystage[:, yi, :], reads=[b_yst[yi]], writes=[], final=True)

        load_x(0)
        stage_a(0)
        for ti, t in enumerate(tiles):
            if ti >= DBG.get("ntiles", 10 ** 9):
                break
            emit_tile(ti, t)
            if t.get("post_flag"):
                fl = prm[:, P_FLAG:P_FLAG + 1]
                S.op("pool", lambda e: e.tensor_scalar(out=hst_p[:], in0=hst_p[:], scalar1=fl, scalar2=None, op0=ALU.mult),
                     reads=b_hst["p"] + [b_prm], writes=b_hst["p"])
                S.op("pool", lambda e: e.tensor_scalar(out=convst_p[:].rearrange("p j s r -> p (j s r)"),
                                                       in0=convst_p[:].rearrange("p j s r -> p (j s r)"),
                                                       scalar1=fl, scalar2=None, op0=ALU.mult),
                     reads=[b_convst["p"], b_prm], writes=[b_convst["p"]])
                S.op("pool", lambda e: e.tensor_scalar(out=ffnst_p[:].rearrange("p f s r -> p (f s r)"),
                                                       in0=ffnst_p[:].rearrange("p f s r -> p (f s r)"),
                                                       scalar1=fl, scalar2=None, op0=ALU.mult),
                     reads=[b_ffnst["p"], b_prm], writes=[b_ffnst["p"]])
            if t["mode"] == "full" and t.get("last"):
                if t.get("sample"):
                    S.dma("sp", oh_s, hst_s[:].rearrange("p j s -> p (j s)"), reads=b_hst["s"], writes=[], final=True)
                    S.dma("sp", oconv_s, convst_s[:].rearrange("p j s r -> p (j s r)"), reads=[b_convst["s"]],
                          writes=[], final=True)
                    S.dma("sp", offn_s, ffnst_s[:].rearrange("p f s r -> p (f s r)"), reads=[b_ffnst["s"]],
                          writes=[], final=True)
                else:
                    S.dma("sp", oh_p, hst_p[:].rearrange("p j s -> p (j s)"), reads=b_hst["p"], writes=[], final=True)
                    S.dma("sp", oconv_p, convst_p[:].rearrange("p j s r -> p (j s r)"), reads=[b_convst["p"]],
                          writes=[], final=True)
                    S.dma("sp", offn_p, ffnst_p[:].rearrange("p f s r -> p (f s r)"), reads=[b_ffnst["p"]],
                          writes=[], final=True)
        assert "ntiles" in DBG or rs["next_use"] == len(units), (rs, len(units))
        S.emit()
    return nc


def _rope_tables(positions):
    half = 8
    inv_freq = np.power(np.float32(THETA), -np.arange(half, dtype=np.float32) * np.float32(2.0 / 16)).astype(np.float32)
    ang = positions.astype(np.float32)[None, :] * inv_freq[:, None]
    cos = np.cos(ang).astype(np.float32)
    sin = np.sin(ang).astype(np.float32)
    npos = positions.shape[0]
    Cd = np.ones((64, npos), np.float32)
    Sd = np.zeros((64, npos), np.float32)
    Cd[0:8] = cos
    Cd[8:16] = cos
    Sd[0:8] = -sin
    Sd[8:16] = sin
    return np.concatenate([Cd, Cd], 0), np.concatenate([Sd, Sd], 0)


def _fm(v, n):
    return np.ascontiguousarray(v.reshape(n, 128).T)


def _prep_weights(inp):
    w_in = inp["w_in"][0]
    perm = np.arange(64)
    perm[0:8] = np.arange(8, 16)
    perm[8:16] = np.arange(0, 8)

    def cols_fc(cols):
        return w_in[:, cols].reshape(8, 128, 128).transpose(1, 0, 2)

    units = []
    for u in range(4):
        units.append([cols_fc(np.arange(C_XR + (2 * u + f) * 128, C_XR + (2 * u + f + 1) * 128)) for f in range(2)])
    for kv in range(4):
        base = C_K + kv * 64
        kc = np.concatenate([base + np.arange(64)] * 2)
        kr = np.concatenate([base + perm] * 2)
        units.append([cols_fc(kc), cols_fc(kr)])
    for j in range(8):
        qc = C_Q + j * 128 + np.arange(128)
        qr = np.concatenate([C_Q + j * 128 + perm, C_Q + j * 128 + 64 + perm])
        units.append([cols_fc(qc), cols_fc(qr)])
    for base in (C_YG, C_GR, C_GA):
        for u in range(4):
            units.append([cols_fc(np.arange(base + (2 * u + f) * 128, base + (2 * u + f + 1) * 128)) for f in range(2)])
    w_in_a = np.stack([np.stack(u, 1) for u in units], 0)
    w_in_a = np.ascontiguousarray(w_in_a.reshape(28, 128, 2048))
    w_v_a = np.ascontiguousarray(w_in[:, C_V:C_V + 256].reshape(8, 128, 256).transpose(1, 0, 2).reshape(128, 2048))

    w_up = inp["w_up"][0]
    uu = []
    for f in range(NF):
        pair = []
        for tt in range(2):
            cols = np.arange(tt * DFF + f * 128, tt * DFF + (f + 1) * 128)
            pair.append(w_up[:, cols].reshape(8, 128, 128).transpose(1, 0, 2))
        uu.append(np.stack(pair, 1))
    w_up_a = np.ascontiguousarray(np.stack(uu, 0).reshape(NF, 128, 2048))

    w_dn = inp["w_down"][0]
    wd = w_dn.reshape(NF, 128, 8, 128)
    dd = []
    for j in range(8):
        for hh in range(2):
            dd.append(wd[hh * 11:(hh + 1) * 11, :, j, :].transpose(1, 0, 2).reshape(128, 1408))
    w_dn_a = np.ascontiguousarray(np.stack(dd, 0))

    w_out_a = np.ascontiguousarray(inp["w_out"][0].reshape(8, 128, D).transpose(1, 0, 2).reshape(128, 8 * D))

    gates = np.zeros((128, 8, 2, 128), np.float32)
    for gi, key in enumerate(("lru_gate_a_w", "lru_gate_i_w")):
        w = inp[key][0]
        for j in range(8):
            for b in range(2):
                gates[b * 64:(b + 1) * 64, j, gi, b * 64:(b + 1) * 64] = w[2 * j + b]
    gates_a = gates.reshape(128, 8 * 2 * 128)

    prm = np.zeros((128, NPRM), np.float32)
    prm[:, P_G1:P_G1 + 8] = _fm(inp["norm_mix"][0], 8)
    cw = inp["lru_conv_w"][0]
    for i in range(4):
        prm[:, P_CW + i:P_CW + 32:4] = _fm(cw[i], 8)
    prm[:, P_CB:P_CB + 8] = _fm(inp["lru_conv_b"][0], 8)
    prm[:, P_BA:P_BA + 8] = _fm(inp["lru_gate_a_b"][0], 8)
    prm[:, P_BI:P_BI + 8] = _fm(inp["lru_gate_i_b"][0], 8)
    prm[:, P_LAM:P_LAM + 8] = _fm(inp["lru_lambda"][0], 8)
    sinks = inp["attn_sinks"][0]
    prm[:, P_SINK:P_SINK + 8] = _fm(np.repeat(sinks, 64), 8)
    prm[:, P_G2:P_G2 + 8] = _fm(inp["norm_ffn"][0], 8)
    fw = inp["ffn_conv_w"][0]
    fb = inp["ffn_conv_b"][0]
    for f in range(NF):
        for tt in range(2):
            fi = 2 * f + tt
            sl = slice(tt * DFF + f * 128, tt * DFF + (f + 1) * 128)
            for i in range(3):
                prm[:, P_FW + 3 * fi + i] = fw[i, sl]
            prm[:, P_FB + fi] = fb[sl]
    gF = np.ascontiguousarray(np.broadcast_to(inp["norm_final"][None, :], (128, D))).astype(np.float32)
    return dict(w_in_a=w_in_a, w_v_a=w_v_a, w_up_a=w_up_a, w_dn_a=w_dn_a, w_out_a=w_out_a, gates_a=gates_a,
                gF=gF), prm


def make_in_maps(inp, cfg, nb=NB):
    shared, prm0 = _prep_weights(inp)
    HALF = cfg.HALF
    in_maps = []
    for c in range(2 * nb):
        b, h = c // 2, c % 2
        m = dict(shared)
        xp = inp["x_prompt"][b]
        m["xmain"] = np.ascontiguousarray(xp[h * HALF:(h + 1) * HALF])
        m["xpre"] = np.ascontiguousarray(xp[0:HALF]) if h == 1 else np.zeros((HALF, D), np.float32)
        s0 = c * SPC
        m["xs"] = np.ascontiguousarray(inp["x_sample"][s0:s0 + SPC].reshape(SPC * LS, D))
        base = h * HALF
        pos = np.concatenate([np.maximum(base - 256 + np.arange(256), 0), base + np.arange(HALF),
                              PAST + np.arange(LS)]).astype(np.int64)
        Ct, St = _rope_tables(pos)
        m["ropeC"], m["ropeS"] = Ct, St
        prm = prm0.copy()
        prm[:, P_FLAG] = float(h)
        if h == 0:
            prm[:, P_HB] = MASK_NEG
            prm[64:128, P_HB + 1] = MASK_NEG
        m["prm"] = prm
        sh = inp["state_lru_h"][0, s0:s0 + SPC]
        m["s_h"] = np.ascontiguousarray(sh.reshape(SPC, 8, 128).transpose(2, 1, 0).reshape(128, 8 * SPC))
        sc = inp["state_lru_conv"][0, s0:s0 + SPC]
        m["s_conv"] = np.ascontiguousarray(sc.reshape(SPC, 3, 8, 128).transpose(3, 2, 0, 1).reshape(128, 8 * SPC * 3))
        sf = inp["state_ffn_conv"][0, s0:s0 + SPC]
        sf = sf.reshape(SPC, 2, 2, NF, 128)
        m["s_ffn"] = np.ascontiguousarray(sf.transpose(4, 3, 2, 0, 1).reshape(128, 44 * SPC * 2))
        ck = inp["cache_attn_k"][0, s0:s0 + SPC]
        kT = ck.transpose(3, 2, 0, 1)
        m["s_kT"] = np.ascontiguousarray(np.concatenate([kT, kT], 0).reshape(128, NKV * SPC * 128))
        m["s_k"] = np.ascontiguousarray(ck.reshape(SPC, 128, 256))
        m["s_v"] = np.ascontiguousarray(inp["cache_attn_v"][0, s0:s0 + SPC].reshape(SPC, 128, 256))
        in_maps.append(m)
    return in_maps


def assemble(results, cfg, nb=NB):
    HALF = cfg.HALF
    seq = 2 * HALF
    ns = 2 * nb * SPC
    y_p = np.zeros((nb, seq, D), np.float32)
    y_s = np.zeros((ns, LS, D), np.float32)
    k_p = np.zeros((1, nb, 128, NKV, 64), np.float32)
    v_p = np.zeros((1, nb, 128, NKV, 64), np.float32)
    h_p = np.zeros((1, nb, D), np.float32)
    lc_p = np.zeros((1, nb, 3, D), np.float32)
    fc_p = np.zeros((1, nb, 2, 2 * DFF), np.float32)
    k_s = np.zeros((1, ns, 128, NKV, 64), np.float32)
    v_s = np.zeros((1, ns, 128, NKV, 64), np.float32)
    h_s = np.zeros((1, ns, D), np.float32)
    lc_s = np.zeros((1, ns, 3, D), np.float32)
    fc_s = np.zeros((1, ns, 2, 2 * DFF), np.float32)
    for c, r in enumerate(results):
        b, h = c // 2, c % 2
        y_p[b, h * HALF:(h + 1) * HALF] = r["y_main"]
        s0 = c * SPC
        y_s[s0:s0 + SPC] = r["y_s"].reshape(SPC, LS, D)
        if h == 1:
            kT = r["okT_p"].reshape(128, NKV, 128)[0:64]
            k_p[0, b] = kT.transpose(2, 1, 0)
            v_p[0, b] = r["ov_p"].reshape(128, NKV, 64)
            h_p[0, b] = r["oh_p"].reshape(128, 8).T.reshape(D)
            lc_p[0, b] = r["oconv_p"].reshape(128, 8, 3).transpose(2, 1, 0).reshape(3, D)
            fc_p[0, b] = r["offn_p"].reshape(128, NF, 2, 2).transpose(3, 2, 1, 0).reshape(2, 2 * DFF)
        kTs = r["okT_s"].reshape(128, NKV, SPC, LS)[0:64]
        k_s[0, s0:s0 + SPC, 0:64] = r["okold_s"].reshape(SPC, 64, NKV, 64)
        k_s[0, s0:s0 + SPC, 64:128] = kTs.transpose(2, 3, 1, 0)
        v_s[0, s0:s0 + SPC, 0:64] = r["ovold_s"].reshape(SPC, 64, NKV, 64)
        v_s[0, s0:s0 + SPC, 64:128] = r["ov_s"].reshape(SPC, LS, NKV, 64)
        h_s[0, s0:s0 + SPC] = r["oh_s"].reshape(128, 8, SPC).transpose(2, 1, 0).reshape(SPC, D)
        lc_s[0, s0:s0 + SPC] = r["oconv_s"].reshape(128, 8, SPC, 3).transpose(2, 3, 1, 0).reshape(SPC, 3, D)
        fc_s[0, s0:s0 + SPC] = r["offn_s"].reshape(128, NF, 2, SPC, 2).transpose(3, 4, 2, 1, 0).reshape(SPC, 2, 2 * DFF)
    return (y_p, y_s, k_p, v_p, h_p, lc_p, fc_p, k_s, v_s, h_s, lc_s, fc_s)


_NC_CACHE = {}


def kernel(**inputs):
    inp = {k: np.asarray(v) for k, v in inputs.items()}
    half = inp["x_prompt"].shape[1] // 2
    nb = inp["x_prompt"].shape[0]
    cfg = Cfg(half)
    if half not in _NC_CACHE:
        _NC_CACHE[half] = build_program(cfg)
    nc = _NC_CACHE[half]
    in_maps = make_in_maps(inp, cfg, nb)
    res = run_bass_kernel_spmd(nc, in_maps, core_ids=list(range(2 * nb)))
    return assemble(res.results, cfg, nb)
```

```python
import contextlib
import numpy as np
import concourse.bass as bass
import concourse.mybir as mybir
from concourse.bass_utils import run_bass_kernel_spmd

F32 = mybir.dt.float32
BF16 = mybir.dt.bfloat16
AF = mybir.ActivationFunctionType
ALU = mybir.AluOpType

D = 1024
NJ = 8
SEQ = 8192
NB = 4
NSAMP = 32
LS = 64
PAST = 1024
DFF = 2816
NF = 22
NKV = 4
THETA = 500000.0
EPS = 1e-6
NCORES = 8
SPC = NSAMP // NCORES
MASK_NEG = -30000.0

C_XR, C_YG, C_Q, C_K, C_V, C_GR, C_GA = 0, 1024, 2048, 3072, 3328, 3584, 4608

P_G1 = 0
P_CW = P_G1 + 8
P_CB = P_CW + 32
P_BA = P_CB + 8
P_BI = P_BA + 8
P_LAM = P_BI + 8
P_SINK = P_LAM + 8
P_G2 = P_SINK + 8
P_FW = P_G2 + 8
P_FB = P_FW + 132
P_FLAG = P_FB + 44
P_HB = P_FLAG + 1
NPRM = P_HB + 2

ENG_NAMES = ("pe", "act", "dve", "pool", "sp")
DBG = {}


class Buf:
    __slots__ = ("name", "last_w", "reads", "chan", "dead")

    def __init__(self, name):
        self.name = name
        self.last_w = None
        self.reads = []
        self.chan = None
        self.dead = False

    def renew(self):
        n = Buf(self.name)
        n.last_w, n.reads, n.chan = self.last_w, list(self.reads), self.chan
        self.dead = True
        return n


class Instr:
    __slots__ = ("eng", "fn", "deps", "is_dma", "chan", "tokval", "needs_inc", "idx", "embed")


class Sched:
    def __init__(self, nc):
        self.nc = nc
        self.instrs = []
        self.prog = {e: [] for e in ENG_NAMES}
        self.nchan = 0
        self.final_dma = []
        ses = DBG.get("same_eng_sync", True)
        self.same_eng_sync = {"pe": False, "act": ses, "dve": ses, "pool": ses, "sp": False}

    def new_chan(self):
        self.nchan += 1
        return self.nchan - 1

    def cbuf(self, name):
        b = Buf(name)
        b.chan = self.new_chan()
        return b

    def op(self, eng, fn, reads=(), writes=(), dma=False, chan=None, noembed=False):
        deps = set()
        for b in list(reads) + list(writes):
            assert not b.dead, f"use of stale rotating buffer {b.name}"
        for b in reads:
            if b.last_w is not None:
                deps.add(b.last_w)
        for b in writes:
            if b.last_w is not None:
                deps.add(b.last_w)
            deps.update(b.reads)
        if dma and chan is None:
            for b in list(writes) + list(reads):
                if b.chan is not None:
                    chan = b.chan
                    break
            assert chan is not None, "dma without channel"
        ins = Instr()
        ins.eng, ins.fn, ins.is_dma, ins.chan = eng, fn, dma, chan
        ins.tokval, ins.needs_inc = None, False
        ins.embed = (not dma) and (not noembed) and eng in ("act", "dve", "pool") and DBG.get("embed_wait", True)
        ins.idx = len(self.instrs)
        best = {}
        for d in deps:
            dd = self.instrs[d]
            k = ("c", dd.chan) if dd.is_dma else ("e", dd.eng)
            if k not in best or best[k] < d:
                best[k] = d
        ins.deps = set(best.values())
        self.instrs.append(ins)
        self.prog[eng].append(ins)
        for b in reads:
            b.reads.append(ins.idx)
        for b in writes:
            b.last_w = ins.idx
            b.reads = []
        return ins.idx

    def dma(self, eng, out_ap, in_ap, reads=(), writes=(), chan=None, final=False):
        def fn(e, out_ap=out_ap, in_ap=in_ap):
            return e.dma_start(out=out_ap, in_=in_ap)
        i = self.op(eng, fn, reads=reads, writes=writes, dma=True, chan=chan)
        if final:
            self.final_dma.append(i)
        return i

    def emit(self):
        nc = self.nc
        instrs = self.instrs

        def skip(dd, ins):
            return dd.eng == ins.eng and (not dd.is_dma) and (not self.same_eng_sync[ins.eng])

        for ins in instrs:
            for d in ins.deps:
                dd = instrs[d]
                if not skip(dd, ins):
                    dd.needs_inc = True
        for i in self.final_dma:
            instrs[i].needs_inc = True
        eng_cnt = {e: 0 for e in ENG_NAMES}
        chan_cnt = [0] * self.nchan
        for ins in instrs:
            if ins.is_dma:
                chan_cnt[ins.chan] += 1
                ins.tokval = 16 * chan_cnt[ins.chan]
            elif ins.needs_inc:
                eng_cnt[ins.eng] += 1
                ins.tokval = eng_cnt[ins.eng]
        self.eng_cnt = eng_cnt
        with contextlib.ExitStack() as st:
            eng_sem = {e: st.enter_context(nc.semaphore(f"s_{e}")) for e in ENG_NAMES if e != "sp"}
            chan_sem = [st.enter_context(nc.semaphore(f"c_{i}")) for i in range(self.nchan)]
            block = st.enter_context(nc.Block())

            def run(engname, e):
                seen = {}
                for ins in self.prog[engname]:
                    need = {}
                    for d in ins.deps:
                        dd = instrs[d]
                        if skip(dd, ins):
                            continue
                        k = ("c", dd.chan) if dd.is_dma else ("e", dd.eng)
                        if dd.tokval > need.get(k, 0):
                            need[k] = dd.tokval
                    waits = []
                    for k, v in need.items():
                        if seen.get(k, 0) >= v:
                            continue
                        seen[k] = v
                        waits.append((chan_sem[k[1]] if k[0] == "c" else eng_sem[k[1]], v))
                    emb = waits.pop() if (ins.embed and waits) else None
                    for h, v in waits:
                        e.wait_ge(h, v)
                    bi = ins.fn(e)
                    if emb is not None:
                        bi._wait_ge(emb[0], emb[1])
                    if ins.is_dma:
                        bi.then_inc(chan_sem[ins.chan], 16)
                    elif ins.needs_inc:
                        bi.then_inc(eng_sem[ins.eng], 1)
                if engname == "sp":
                    fin = {}
                    for i in self.final_dma:
                        dd = instrs[i]
                        fin[dd.chan] = max(fin.get(dd.chan, 0), dd.tokval)
                    for c, v in fin.items():
                        e.wait_ge(chan_sem[c], v)

            @block.tensor
            def _(e):
                run("pe", e)

            @block.scalar
            def _(e):
                run("act", e)

            @block.vector
            def _(e):
                run("dve", e)

            @block.gpsimd
            def _(e):
                run("pool", e)

            @block.sync
            def _(e):
                run("sp", e)


class Cfg:
    def __init__(self, half=4096):
        self.HALF = half
        self.WARM = 128
        self.PRE = half - self.WARM
        self.NPOS = 256 + half + LS
        self.TS = SPC * LS


def tile_list(cfg):
    tiles = []
    r = 0
    while r < cfg.PRE:
        T = min(512, cfg.PRE - r)
        tiles.append(dict(mode="pre", src="xpre", row0=r, T=T, nseg=1, L=T, run="p",
                          kv_last=(r + T == cfg.PRE), rope_off=256 - cfg.WARM - 128))
        r += T
    tiles.append(dict(mode="warm", src="xpre", row0=cfg.PRE, T=cfg.WARM, nseg=1, L=cfg.WARM, run="p",
                      rope_off=256 - cfg.WARM, post_flag=True))
    nmain = cfg.HALF // 512
    for i in range(nmain):
        tiles.append(dict(mode="full", src="xmain", row0=i * 512, T=512, nseg=1, L=512, run="p",
                          rope_off=256 + i * 512, first=(i == 0), last=(i == nmain - 1), ydst="y_main"))
    tiles.append(dict(mode="full", src="xs", row0=0, T=cfg.TS, nseg=SPC, L=LS, run="s",
                      rope_off=256 + cfg.HALF, sample=True, last=True, ydst="y_s"))
    return tiles


def kx_early(tiles, i):
    if i < 1 or i >= len(tiles):
        return False
    a, b = tiles[i - 1], tiles[i]
    return (a["mode"] == "full" and b["mode"] == "full" and not a.get("sample") and not b.get("sample"))


def tile_units(t, early=False, next_early=False):
    if t["mode"] == "pre":
        us = [("i", u) for u in range(4)]
        if t.get("kv_last"):
            us += [("i", u) for u in range(4, 8)]
        return us
    us = [] if early else [("i", u) for u in range(4, 8)] + [("i", u) for u in range(4)]
    us += [("i", u) for u in range(12 if early else 8, 16)]
    us += [("i", u) for u in (16, 17, 20, 21, 18, 19, 22, 23)]
    us += [("i", u) for u in range(24, 28)]
    us += [("o", u) for u in range(4)]
    us += [("u", f) for f in range(NF)]
    if t["mode"] == "full":
        for j in range(8):
            us += [("d", 2 * j), ("d", 2 * j + 1)]
            if next_early:
                us.append(("i", 4 + j) if j < 4 else ("i", j - 4))
        if next_early:
            us += [("i", u) for u in range(8, 12)]
    return us


def build_program(cfg):
    nc = bass.Bass("TRN2", target_bir_lowering=False)
    S = Sched(nc)
    HALF = cfg.HALF

    def din(name, shape, dt=F32):
        return nc.dram_tensor(name, list(shape), dt, kind="ExternalInput").ap()

    def dout(name, shape, dt=F32):
        return nc.dram_tensor(name, list(shape), dt, kind="ExternalOutput").ap()

    xsrc = {"xpre": din("xpre", [HALF, D]), "xmain": din("xmain", [HALF, D]), "xs": din("xs", [cfg.TS, D])}
    ropeC = din("ropeC", [128, cfg.NPOS])
    ropeS = din("ropeS", [128, cfg.NPOS])
    prm_d = din("prm", [128, NPRM])
    gF_d = din("gF", [128, D])
    w_in_d = din("w_in_a", [28, 128, 2048])
    w_up_d = din("w_up_a", [NF, 128, 2048])
    w_dn_d = din("w_dn_a", [16, 128, 1408])
    w_out_d = din("w_out_a", [128, 8 * D])
    w_v_d = din("w_v_a", [128, 8 * 256])
    gates_d = din("gates_a", [128, 8 * 2 * 128])
    s_h_d = din("s_h", [128, 8 * SPC])
    s_conv_d = din("s_conv", [128, 8 * SPC * 3])
    s_ffn_d = din("s_ffn", [128, 44 * SPC * 2])
    s_kT_d = din("s_kT", [128, NKV * SPC * 128])
    s_v_d = din("s_v", [SPC, 128, 256])
    s_k_d = din("s_k", [SPC, 128, 256])

    ydst = {"y_main": dout("y_main", [HALF, D]), "y_s": dout("y_s", [cfg.TS, D])}
    okT_p = dout("okT_p", [128, NKV * 128])
    ov_p = dout("ov_p", [128, 256])
    oh_p = dout("oh_p", [128, 8])
    oconv_p = dout("oconv_p", [128, 8 * 3])
    offn_p = dout("offn_p", [128, 44 * 2])
    okT_s = dout("okT_s", [128, NKV * cfg.TS])
    ov_s = dout("ov_s", [cfg.TS, 256])
    okold_s = dout("okold_s", [SPC, 64, 256])
    ovold_s = dout("ovold_s", [SPC, 64, 256])
    oh_s = dout("oh_s", [128, 8 * SPC])
    oconv_s = dout("oconv_s", [128, 8 * SPC * 3])
    offn_s = dout("offn_s", [128, 44 * SPC * 2])

    wi_s = nc.dram_tensor("wi_s", [28, 128, 2048], BF16).ap()
    wu_s = nc.dram_tensor("wu_s", [NF, 128, 2048], BF16).ap()
    wd_s = nc.dram_tensor("wd_s", [16, 128, 1408], BF16).ap()
    wo_s = nc.dram_tensor("wo_s", [4, 128, 2048], BF16).ap()

    tiles = tile_list(cfg)
    units = []
    nt_lim = DBG.get("ntiles", 10 ** 9)
    for i, t in enumerate(tiles):
        units += tile_units(t, early=kx_early(tiles, i), next_early=(kx_early(tiles, i + 1) and i + 1 < nt_lim))

    with contextlib.ExitStack() as st:
        def sb(name, shape, dt):
            return st.enter_context(nc.sbuf_tensor(name, list(shape), dt))

        def ps(name, shape, dt):
            return st.enter_context(nc.psum_tensor(name, list(shape), dt))

        x_sb2 = sb("x_sb2", [128, 2, 4, D], F32)
        acc_all = sb("acc_all", [128, 8, 512], F32)
        xn_tok = sb("xn_tok", [128, 4, D], BF16)
        xnT = sb("xnT", [128, 8, 512], BF16)
        rnn = sb("rnn", [128, 8, 512], F32)
        KT0 = sb("KT0", [128, NKV, 768], BF16)
        KT1 = sb("KT1", [128, NKV, 768], BF16)
        KTz = (KT0, KT1)
        V_all = sb("V_all", [128, 6, 256], BF16)
        V2 = sb("V2", [128, 6, 256], BF16)
        ptown = sb("ptown", [128, 2, 2, 256], BF16)
        R1 = sb("R1", [128, 22 * 512], BF16)
        NTMP = 12
        tmpf = sb("tmpf", [128, NTMP, 520], F32)
        NTB = 4
        tmpb = sb("tmpb", [128, NTB, 512], BF16)
        NTG = 4
        tmpg = sb("tmpg", [128, NTG, 520], F32)
        ystage = sb("ystage", [128, 1, D], F32)
        gates_sb = sb("gates_sb", [128, 8, 2, 128], BF16)
        w_v_sb = sb("w_v_sb", [128, 8, 256], BF16)
        NR = 6
        ring = sb("ring", [128, NR, 2048], BF16)
        prm = sb("prm_sb", [128, NPRM], F32)
        gF = sb("gF_sb", [128, D], F32)
        cst = sb("cst", [128, 64], F32)
        ident_f = sb("ident_f", [128, 128], F32)
        ident_b = sb("ident_b", [128, 128], BF16)
        ones_b = sb("ones_b", [128, 64], BF16)
        stat = sb("stat", [128, 16], F32)
        hst_p = sb("hst_p", [128, 8, 1], F32)
        hst_s = sb("hst_s", [128, 8, SPC], F32)
        convst_p = sb("convst_p", [128, 8, 1, 3], F32)
        convst_s = sb("convst_s", [128, 8, SPC, 3], F32)
        ffnst_p = sb("ffnst_p", [128, 44, 1, 2], F32)
        ffnst_s = sb("ffnst_s", [128, 44, SPC, 2], F32)

        QT = R1[:, 0:8 * 512].rearrange("p (j t) -> p j t", j=8)
        mixedT = R1[:, 8 * 512:16 * 512].rearrange("p (j t) -> p j t", j=8)
        hT = R1[:, 0:22 * 512].rearrange("p (f t) -> p f t", f=NF)
        CS = sb("CS", [128, 2, 512], F32)

        PG = ps("PG", [128, 3, 512], F32)
        PH = ps("PH", [128, 4, 512], F32)
        TP = ps("TP", [128, 2, 512], BF16)

        B = {}

        def nb(name, chan=False):
            B[name] = S.cbuf(name) if chan else Buf(name)
            return B[name]

        bx2 = [[nb(f"x{i}_{g}", chan=True) for g in range(4)] for i in range(2)]
        b_acc_all = [nb(f"acc_all{j}") for j in range(8)]
        b_xn_tok = [nb(f"xn_tok{g}") for g in range(4)]
        b_xnT = [nb(f"xnT{j}") for j in range(8)]
        b_rnn = [nb(f"rnn{j}") for j in range(8)]
        b_KT = nb("KT", chan=True)
        b_V = nb("V_all", chan=True)
        b_ptown = [[nb(f"ptown{h}{i}") for i in range(2)] for h in range(2)]
        b_QT = [nb(f"QT{j}") for j in range(8)]
        b_mix = [nb(f"mix{j}") for j in range(8)]
        b_CS = nb("CS", chan=True)
        b_tmpf = [nb(f"tf{i}", chan=True) for i in range(NTMP)]
        b_tmpb = [nb(f"tb{i}") for i in range(NTB)]
        b_tmpg = [nb(f"tg{i}") for i in range(NTG)]
        b_yst = [nb(f"yst{i}", chan=True) for i in range(1)]
        b_gates = nb("gates", chan=True)
        b_wv = nb("wv", chan=True)
        b_ring = [nb(f"ring{i}", chan=True) for i in range(NR)]
        b_prm = nb("prm", chan=True)
        b_gF = nb("gF", chan=True)
        b_cst = nb("cst")
        b_ident = nb("ident")
        b_stat = nb("stat")
        b_hst = {"p": [nb(f"hst_p{j}", chan=(j == 0)) for j in range(8)],
                 "s": [nb(f"hst_s{j}", chan=(j == 0)) for j in range(8)]}
        b_convst = {"p": nb("convst_p", chan=True), "s": nb("convst_s", chan=True)}
        b_ffnst = {"p": nb("ffnst_p", chan=True), "s": nb("ffnst_s", chan=True)}
        b_PG = [nb(f"PG{i}") for i in range(3)]
        b_PH = [nb(f"PH{i}") for i in range(4)]
        b_TP = nb("TP")
        b_out = nb("dram_out")
        b_in = nb("dram_in")
        b_hT_all = b_QT + b_mix

        hst = {"p": hst_p, "s": hst_s}
        convst = {"p": convst_p, "s": convst_s}
        ffnst = {"p": ffnst_p, "s": ffnst_s}

        cnt = {"tf": 0, "tb": 0, "pg": 0, "yst": 0, "tg": 0}

        def tf():
            i = cnt["tf"] % NTMP
            cnt["tf"] += 1
            b_tmpf[i] = b_tmpf[i].renew()
            return tmpf[:, i, :], b_tmpf[i]

        def tg():
            i = cnt["tg"] % NTG
            cnt["tg"] += 1
            b_tmpg[i] = b_tmpg[i].renew()
            return tmpg[:, i, :], b_tmpg[i]

        def tb():
            i = cnt["tb"] % NTB
            cnt["tb"] += 1
            b_tmpb[i] = b_tmpb[i].renew()
            return tmpb[:, i, :], b_tmpb[i]

        def pg():
            i = cnt["pg"] % 3
            cnt["pg"] += 1
            b_PG[i] = b_PG[i].renew()
            return PG[:, i, :], b_PG[i]

        S.dma("sp", prm[:], prm_d, reads=[b_in], writes=[b_prm])
        S.dma("sp", gF[:], gF_d, reads=[b_in], writes=[b_gF])
        cast_buf = {}
        pending_casts = []

        def add_cast(keys, dst, src):
            b = S.cbuf("cast_" + str(keys[0]))
            for k in keys:
                cast_buf[k] = (b, False)
            pending_casts.append((keys, dst, src, b))

        add_cast([("i", u) for u in range(0, 4)], wi_s[0:4], w_in_d[0:4])
        for u0 in range(4, 28, 2):
            add_cast([("i", u) for u in range(u0, u0 + 2)], wi_s[u0:u0 + 2], w_in_d[u0:u0 + 2])
        wo_v = wo_s.rearrange("u p c -> p u c")
        wo_i = w_out_d.rearrange("p (u c) -> p u c", u=4)
        for u0 in range(0, 4, 2):
            add_cast([("o", u) for u in range(u0, u0 + 2)], wo_v[:, u0:u0 + 2, :], wo_i[:, u0:u0 + 2, :])
        for u0 in range(0, NF, 2):
            add_cast([("u", u) for u in range(u0, u0 + 2)], wu_s[u0:u0 + 2], w_up_d[u0:u0 + 2])
        for u0 in range(0, 16, 2):
            add_cast([("d", u) for u in range(u0, u0 + 2)], wd_s[u0:u0 + 2], w_dn_d[u0:u0 + 2])

        def cast_some(n):
            for _ in range(n):
                if not pending_casts:
                    return
                keys, dst, src, b = pending_casts.pop(0)
                S.dma("pool", dst, src, reads=[b_in], writes=[b])
                for k in keys:
                    cast_buf[k] = (b, True)

        def cast_until(key):
            while not cast_buf[key][1]:
                cast_some(1)

        S.op("pool", lambda e: e.memset(ident_f[:], 0.0), writes=[b_ident])
        S.op("pool", lambda e: e.affine_select(out=ident_f[:], in_=ident_f[:], pattern=[[-1, 128]],
                                               compare_op=ALU.not_equal, fill=1.0, base=0,
                                               channel_multiplier=1), reads=[b_ident], writes=[b_ident])
        S.op("pool", lambda e: e.tensor_copy(out=ident_b[:], in_=ident_f[:]), reads=[b_ident], writes=[b_ident])
        S.op("pool", lambda e: e.memset(ones_b[:], 1.0), writes=[b_ident])
        S.op("pool", lambda e: e.memset(KT0[:], 0.0), writes=[b_KT])
        S.op("pool", lambda e: e.memset(KT1[:], 0.0), writes=[b_KT])
        S.op("pool", lambda e: e.memset(V_all[:], 0.0), writes=[b_V])
        S.op("pool", lambda e: e.memset(V2[:], 0.0), writes=[b_V])
        S.op("pool", lambda e: e.memset(ptown[:], 0.0), writes=[b_ptown[0][0], b_ptown[0][1], b_ptown[1][0], b_ptown[1][1]])
        S.op("pool", lambda e: e.memset(hst_p[:], 0.0), writes=b_hst["p"])
        S.op("pool", lambda e: e.memset(convst_p[:], 0.0), writes=[b_convst["p"]])
        S.op("pool", lambda e: e.memset(ffnst_p[:], 0.0), writes=[b_ffnst["p"]])
        C_HCL, C_ES, C_HBA, C_HBI, C_NEGH, C_SCR, C_CL = 0, 8, 16, 24, 32, 40, 56
        S.op("pool", lambda e: e.memset(cst[:, C_NEGH:C_NEGH + 8], -0.5), writes=[b_cst])
        ev = cst[:, C_SCR:C_SCR + 8]
        e2 = cst[:, C_SCR + 8:C_SCR + 16]
        S.op("act", lambda e: e.activation(out=ev, in_=prm[:, P_LAM:P_LAM + 8], func=AF.Exp, scale=-1.0),
             reads=[b_prm], writes=[b_cst])
        S.op("dve", lambda e: e.tensor_scalar(out=e2, in0=ev, scalar1=-0.25, scalar2=1.0 / 3.0,
                                              op0=ALU.mult, op1=ALU.add), reads=[b_cst], writes=[b_cst])
        S.op("dve", lambda e: e.tensor_tensor(out=e2, in0=e2, in1=ev, op=ALU.mult), reads=[b_cst], writes=[b_cst])
        S.op("dve", lambda e: e.tensor_scalar(out=e2, in0=e2, scalar1=-1.0, scalar2=0.5,
                                              op0=ALU.mult, op1=ALU.add), reads=[b_cst], writes=[b_cst])
        S.op("dve", lambda e: e.tensor_tensor(out=e2, in0=e2, in1=ev, op=ALU.mult), reads=[b_cst], writes=[b_cst])
        S.op("dve", lambda e: e.tensor_scalar(out=e2, in0=e2, scalar1=-1.0, scalar2=1.0,
                                              op0=ALU.mult, op1=ALU.add), reads=[b_cst], writes=[b_cst])
        S.op("dve", lambda e: e.tensor_tensor(out=e2, in0=e2, in1=ev, op=ALU.mult), reads=[b_cst], writes=[b_cst])
        S.op("dve", lambda e: e.tensor_scalar(out=cst[:, C_HCL:C_HCL + 8], in0=e2, scalar1=-4.0, scalar2=None,
                                              op0=ALU.mult), reads=[b_cst], writes=[b_cst])
        S.op("dve", lambda e: e.tensor_scalar(out=cst[:, C_CL:C_CL + 8], in0=e2, scalar1=-8.0, scalar2=None,
                                              op0=ALU.mult), reads=[b_cst], writes=[b_cst])
        S.op("act", lambda e: e.activation(out=cst[:, C_ES:C_ES + 8], in_=prm[:, P_SINK:P_SINK + 8], func=AF.Exp),
             reads=[b_prm], writes=[b_cst])
        S.op("dve", lambda e: e.tensor_scalar(out=cst[:, C_HBA:C_HBA + 8], in0=prm[:, P_BA:P_BA + 8], scalar1=0.5,
                                              scalar2=None, op0=ALU.mult), reads=[b_prm, b_cst], writes=[b_cst])
        S.op("dve", lambda e: e.tensor_scalar(out=cst[:, C_HBI:C_HBI + 8], in0=prm[:, P_BI:P_BI + 8], scalar1=0.5,
                                              scalar2=None, op0=ALU.mult), reads=[b_prm, b_cst], writes=[b_cst])

        cast_some(1)
        S.dma("pool", gates_sb[:].rearrange("p j a c -> p (j a c)"), gates_d, reads=[b_in], writes=[b_gates])
        S.dma("pool", w_v_sb[:].rearrange("p k c -> p (k c)"), w_v_d, reads=[b_in], writes=[b_wv])
        S.dma("sp", hst_s[:].rearrange("p j s -> p (j s)"), s_h_d, reads=[b_in], writes=b_hst["s"])
        S.dma("sp", convst_s[:].rearrange("p j s r -> p (j s r)"), s_conv_d, reads=[b_in], writes=[b_convst["s"]])
        S.dma("sp", ffnst_s[:].rearrange("p f s r -> p (f s r)"), s_ffn_d, reads=[b_in], writes=[b_ffnst["s"]])
        ch_misc = S.new_chan()
        if not DBG.get("no_misc"):
          S.dma("sp", okold_s, s_k_d[:, 64:128, :], reads=[b_in], writes=[], chan=ch_misc, final=True)
          S.dma("sp", ovold_s, s_v_d[:, 64:128, :], reads=[b_in], writes=[], chan=ch_misc, final=True)

        rs = {"next_dma": 0, "next_use": 0}

        def unit_src(u):
            kind, i = u
            cast_until(u)
            bsrc = cast_buf[u][0]
            if kind == "i":
                return wi_s[i], 2048, bsrc
            if kind == "u":
                return wu_s[i], 2048, bsrc
            if kind == "o":
                return wo_s[i], 2048, bsrc
            return wd_s[i], 1408, bsrc

        def ring_issue():
            n = rs["next_dma"]
            if n >= len(units):
                return
            src, w, bsrc = unit_src(units[n])
            slot = n % NR
            S.dma("sp", ring[:, slot, 0:w], src, reads=[bsrc], writes=[b_ring[slot]])
            rs["next_dma"] += 1

        def ring_take(expect):
            n = rs["next_use"]
            assert units[n] == expect, (units[n], expect)
            assert rs["next_dma"] > n
            rs["next_use"] += 1
            slot = n % NR
            return ring[:, slot, :], b_ring[slot]

        for _ in range(NR):
            if DBG.get("no_ring"):
                break
            ring_issue()

        def norm_T(x_sb, bx, T, gcol):
            NG = T // 128
            for g in range(NG):
                S.op("act", lambda e, g=g: e.activation(out=xn_tok[:, g, :], in_=x_sb[:, g, :], func=AF.Square,
                                                        accum_out=stat[:, g:g + 1]),
                     reads=[bx[g]], writes=[b_xn_tok[g], b_stat], noembed=True)
            S.op("dve", lambda e: e.tensor_scalar(out=stat[:, 4:4 + NG], in0=stat[:, 0:NG], scalar1=1.0 / D,
                                                  scalar2=EPS, op0=ALU.mult, op1=ALU.add),
                 reads=[b_stat], writes=[b_stat])
            S.op("act", lambda e: e.activation(out=stat[:, 12:12 + NG], in_=stat[:, 4:4 + NG], func=AF.Sqrt),
                 reads=[b_stat], writes=[b_stat])
            S.op("dve", lambda e: e.reciprocal(out=stat[:, 8:8 + NG], in_=stat[:, 12:12 + NG]),
                 reads=[b_stat], writes=[b_stat])
            for g in range(NG):
                S.op("dve", lambda e, g=g: e.tensor_scalar(out=xn_tok[:, g, :], in0=x_sb[:, g, :],
                                                           scalar1=stat[:, 8 + g:9 + g], scalar2=None, op0=ALU.mult),
                     reads=[bx[g], b_stat], writes=[b_xn_tok[g]])
            for r in range(4):
                def tr(e, r=r):
                    last = None
                    for jj in range(2):
                        j = 2 * r + jj
                        for g in range(NG):
                            last = e.transpose(out=TP[:, jj, g * 128:(g + 1) * 128],
                                               in_=xn_tok[:, g, j * 128:(j + 1) * 128], identity=ident_b[:])
                    return last
                S.op("pe", tr, reads=b_xn_tok[:NG] + [b_ident], writes=[b_TP])
                for jj in range(2):
                    j = 2 * r + jj
                    S.op("act", lambda e, j=j, jj=jj: e.activation(out=xnT[:, j, 0:T], in_=TP[:, jj, 0:T], func=AF.Copy,
                                                                   scale=prm[:, gcol + j:gcol + j + 1]),
                         reads=[b_TP, b_prm], writes=[b_xnT[j]])

        def mm_fm(out_ap, wview, f, c0, c1, reads, writes, rhs=None):
            def fn(e):
                last = None
                for k in range(8):
                    last = e.matmul(out_ap, lhsT=wview[:, f, k, :], rhs=xnT[:, k, c0:c1], start=(k == 0), stop=(k == 7))
                return last
            S.op("pe", fn, reads=reads + b_xnT, writes=writes)

        def unit_view(slot_ap):
            return slot_ap.rearrange("p (f k c) -> p f k c", f=2, k=8)

        def lru_a(run, j, xr_ps, b_xr_ps, T, nseg, L):
            W = 3 + L
            xr, b_xr = tf()
            xr3 = xr[:, 0:nseg * W].rearrange("p (s c) -> p s c", c=W)
            cs_ = convst[run]
            S.op("act", lambda e: e.activation(out=xr3[:, :, 3:3 + L], in_=xr_ps.rearrange("p (s l) -> p s l", l=L),
                                               func=AF.Copy), reads=[b_xr_ps], writes=[b_xr])
            S.op("pool", lambda e: e.tensor_copy(out=xr3[:, :, 0:3], in_=cs_[:, j, :, :]),
                 reads=[b_convst[run]], writes=[b_xr])
            acc, b_acc = acc_all[:, j, :], b_acc_all[j]
            acc3 = acc[:, 0:T].rearrange("p (s l) -> p s l", l=L)
            cw = lambda i: prm[:, P_CW + 4 * j + i:P_CW + 4 * j + i + 1]
            S.op("act", lambda e: e.activation(out=acc3, in_=xr_ps.rearrange("p (s l) -> p s l", l=L), func=AF.Identity,
                                               scale=cw(3), bias=prm[:, P_CB + j:P_CB + j + 1]),
                 reads=[b_xr_ps, b_prm], writes=[b_acc])
            for i in range(0, 3):
                S.op("dve", lambda e, i=i: e.scalar_tensor_tensor(out=acc3, in0=xr3[:, :, i:i + L], scalar=cw(i), in1=acc3,
                                                                  op0=ALU.mult, op1=ALU.add),
                     reads=[b_xr, b_prm, b_acc], writes=[b_acc])
            S.op("pool", lambda e: e.tensor_copy(out=cs_[:, j, :, :], in_=xr3[:, :, L:L + 3]),
                 reads=[b_xr], writes=[b_convst[run]])

        def lru_stages(run, js, T, nseg, L):
            st_ = {}
            aux_eng = "pool" if run_is_pre[0] else "dve"
            hs = hst[run]

            def s0():
                for j in js:
                    acc, b_acc = acc_all[:, j, :], b_acc_all[j]
                    xcb, b_xcb = tb()
                    S.op(aux_eng, lambda e, xcb=xcb, acc=acc: e.tensor_copy(out=xcb[:, 0:T], in_=acc[:, 0:T]),
                         reads=[b_acc], writes=[b_xcb])
                    st_[j] = dict(acc=acc, b_acc=b_acc, xcb=xcb, b_xcb=b_xcb)

            def s1():
                for j in js:
                    d = st_[j]
                    xcb, b_xcb, acc, b_acc = d["xcb"], d["b_xcb"], d["acc"], d["b_acc"]
                    ga, b_ga = pg()
                    S.op("pe", lambda e, ga=ga, xcb=xcb, j=j: e.matmul(ga[:, 0:T], lhsT=gates_sb[:, j, 0, :], rhs=xcb[:, 0:T],
                                                                       start=True, stop=True),
                         reads=[b_gates, b_xcb], writes=[b_ga])
                    gi, b_gi = pg()
                    S.op("pe", lambda e, gi=gi, xcb=xcb, j=j: e.matmul(gi[:, 0:T], lhsT=gates_sb[:, j, 1, :], rhs=xcb[:, 0:T],
                                                                       start=True, stop=True),
                         reads=[b_gates, b_xcb], writes=[b_gi])
                    rp, b_rp = tf()
                    ip, b_ip = tg()
                    S.op("act", lambda e, rp=rp, ga=ga, j=j: e.activation(out=rp[:, 0:T], in_=ga[:, 0:T], func=AF.Tanh,
                                                                          scale=0.5, bias=cst[:, C_HBA + j:C_HBA + j + 1]),
                         reads=[b_ga, b_cst], writes=[b_rp])
                    S.op("act", lambda e, ip=ip, gi=gi, j=j: e.activation(out=ip[:, 0:T], in_=gi[:, 0:T], func=AF.Tanh,
                                                                          scale=0.5, bias=cst[:, C_HBI + j:C_HBI + j + 1]),
                         reads=[b_gi, b_cst], writes=[b_ip])
                    S.op("dve", lambda e, ip=ip, acc=acc: e.scalar_tensor_tensor(out=acc[:, 0:T], in0=ip[:, 0:T], scalar=1.0,
                                                                                 in1=acc[:, 0:T], op0=ALU.add, op1=ALU.mult),
                         reads=[b_ip, b_acc], writes=[b_acc])
                    d.update(rp=rp, b_rp=b_rp)

            def s2():
                for j in js:
                    d = st_[j]
                    rp, b_rp = d["rp"], d["b_rp"]
                    S.op("act", lambda e, rp=rp, j=j: e.activation(out=rp[:, 0:T], in_=rp[:, 0:T], func=AF.Exp,
                                                                   scale=cst[:, C_HCL + j:C_HCL + j + 1],
                                                                   bias=cst[:, C_HCL + j:C_HCL + j + 1]),
                         reads=[b_rp, b_cst], writes=[b_rp])

            def s3():
                for j in js:
                    d = st_[j]
                    rp, b_rp = d["rp"], d["b_rp"]
                    S.op(aux_eng, lambda e, rp=rp, j=j: e.tensor_tensor(out=rnn[:, j, 0:T], in0=rp[:, 0:T], in1=rp[:, 0:T],
                                                                        op=ALU.mult),
                         reads=[b_rp], writes=[b_rnn[j]])

            def s4():
                for j in js:
                    S.op("act", lambda e, j=j: e.activation(out=rnn[:, j, 0:T], in_=rnn[:, j, 0:T], func=AF.Sqrt,
                                                            scale=-1.0, bias=1.0),
                         reads=[b_rnn[j]], writes=[b_rnn[j]])

            def s5():
                for j in js:
                    d = st_[j]
                    acc, b_acc = d["acc"], d["b_acc"]
                    S.op("dve", lambda e, acc=acc, j=j: e.scalar_tensor_tensor(out=acc[:, 0:T], in0=acc[:, 0:T], scalar=0.5,
                                                                               in1=rnn[:, j, 0:T], op0=ALU.mult, op1=ALU.mult),
                         reads=[b_acc, b_rnn[j]], writes=[b_acc])

            def s6():
                for j in js:
                    d = st_[j]
                    rp, b_rp, acc, b_acc = d["rp"], d["b_rp"], d["acc"], d["b_acc"]
                    for s in range(nseg):
                        S.op("dve", lambda e, s=s, j=j, rp=rp, acc=acc: e.tensor_tensor_scan(
                            out=rnn[:, j, s * L:(s + 1) * L], data0=rp[:, s * L:(s + 1) * L],
                            data1=acc[:, s * L:(s + 1) * L], initial=hs[:, j, s:s + 1], op0=ALU.mult, op1=ALU.add),
                             reads=[b_rp, b_acc, b_hst[run][j]], writes=[b_rnn[j]])

            def s7():
                for j in js:
                    S.op("pool", lambda e, j=j: e.tensor_copy(
                        out=hs[:, j, :], in_=rnn[:, j, 0:T].rearrange("p (s l) -> p s l", l=L)[:, :, L - 1]),
                         reads=[b_rnn[j]], writes=[b_hst[run][j]])

            return [s0, s1, s2, s3, s4, s5, s6, s7]

        def lru_b_chains(run, T, nseg, L, fillers=(), offset=2):
            g1 = lru_stages(run, [0, 1, 2, 3], T, nseg, L)
            g2 = lru_stages(run, [4, 5, 6, 7], T, nseg, L)
            fillers = list(fillers)
            nst = len(g1)
            for step in range(nst + offset):
                if step < nst:
                    g1[step]()
                if 0 <= step - offset < nst:
                    g2[step - offset]()
                if fillers and fillers[0][0] <= step:
                    fillers.pop(0)[1]()
            while fillers:
                fillers.pop(0)[1]()

        run_is_pre = [False]

        def rope_pair(p_ps, b_p, r_ps, b_r, T, b_dst, out_views, f32_dma=None, alloc=None):
            alloc = alloc or tf
            t1, b_t1 = alloc()
            t2, b_t2 = alloc()
            S.op("dve", lambda e: e.tensor_tensor(out=t1[:, 0:T], in0=p_ps, in1=CS[:, 0, 0:T], op=ALU.mult),
                 reads=[b_p, b_CS], writes=[b_t1])
            S.op("dve", lambda e: e.tensor_tensor(out=t2[:, 0:T], in0=r_ps, in1=CS[:, 1, 0:T], op=ALU.mult),
                 reads=[b_r, b_CS], writes=[b_t2])
            if f32_dma is not None:
                S.op("pool", lambda e: e.tensor_tensor(out=t1[:, 0:T], in0=t1[:, 0:T], in1=t2[:, 0:T], op=ALU.add),
                     reads=[b_t1, b_t2], writes=[b_t1])
                for dst, (p0, p1), (c0, c1) in out_views:
                    S.op("pool", lambda e, dst=dst, p0=p0, p1=p1, c0=c0, c1=c1: e.tensor_copy(out=dst, in_=t1[p0:p1, c0:c1]),
                         reads=[b_t1], writes=[b_dst])
                dram_ap, (c0, c1) = f32_dma
                S.dma("sp", dram_ap, t1[:, c0:c1], reads=[b_t1], writes=[], final=True)
            else:
                for dst, (p0, p1), (c0, c1) in out_views:
                    S.op("pool", lambda e, dst=dst, p0=p0, p1=p1, c0=c0, c1=c1: e.tensor_tensor(
                        out=dst, in0=t1[p0:p1, c0:c1], in1=t2[p0:p1, c0:c1], op=ALU.add),
                         reads=[b_t1, b_t2], writes=[b_dst])

        def x_of(ti):
            return x_sb2[:, ti % 2], bx2[ti % 2]

        def load_x(ti):
            if ti >= len(tiles) or ti >= DBG.get("ntiles", 10 ** 9):
                return
            t = tiles[ti]
            if t.get("x_loaded"):
                return
            t["x_loaded"] = True
            xs, bxs = x_of(ti)
            src = xsrc[t["src"]]
            for g in range(t["T"] // 128):
                r0 = t["row0"] + g * 128
                S.dma("sp", xs[:, g, :], src[r0:r0 + 128, :], reads=[b_in], writes=[bxs[g]])

        def load_cs(t):
            if t.get("cs_done"):
                return
            t["cs_done"] = True
            if not (t["mode"] != "pre" or t.get("kv_last")):
                return
            off, T_, L_ = t["rope_off"], t["T"], t["L"]
            if t.get("sample"):
                for s in range(t["nseg"]):
                    S.dma("sp", CS[:, 0, s * L_:(s + 1) * L_], ropeC[:, off:off + L_], reads=[b_in], writes=[b_CS])
                    S.dma("sp", CS[:, 1, s * L_:(s + 1) * L_], ropeS[:, off:off + L_], reads=[b_in], writes=[b_CS])
            else:
                w = T_ if t["mode"] != "pre" else 128
                S.dma("sp", CS[:, 0, 0:w], ropeC[:, off:off + w], reads=[b_in], writes=[b_CS])
                S.dma("sp", CS[:, 1, 0:w], ropeS[:, off:off + w], reads=[b_in], writes=[b_CS])

        def stage_a(ti):
            xs, bxs = x_of(ti)
            norm_T(xs, bxs, tiles[ti]["T"], P_G1)

        def emit_tile(ti, t):
            mode, T, nseg, L, run = t["mode"], t["T"], t["nseg"], t["L"], t["run"]
            run_is_pre[0] = (mode == "pre")
            NG = T // 128
            ncs = L // 64
            nch = T // 64
            full = mode != "pre"
            sample = t.get("sample", False)
            x_sb, bx = x_of(ti)
            nxt = tiles[ti + 1] if (ti + 1 < len(tiles) and ti + 1 < DBG.get("ntiles", 10 ** 9)) else None

            load_x(ti + 1)
            if mode == "pre":
                load_x(ti + 2)
            if not t.get("cs_done"):
                load_cs(t)
            if sample:
                skv = s_kT_d.rearrange("p (k s c) -> p k s c", k=NKV, s=nseg)
                for z in range(2):
                    ktv = KTz[z][:, :, 0:nseg * 192].rearrange("p k (s c) -> p k s c", c=192)
                    for kv in range(NKV):
                        S.dma("pool", ktv[z * 64:(z + 1) * 64, kv, :, 0:128], skv[z * 64:(z + 1) * 64, kv, :, :],
                              reads=[b_in], writes=[b_KT])
                S.dma("pool", V_all[:, 0:nseg, :], s_v_d.rearrange("s p c -> p s c"), reads=[b_in], writes=[b_V])

            def kt_col(seg, slot):
                return (seg * (2 + ncs) + slot) * 64

            hslot = (lambda s: s) if sample else (lambda s: 0)
            own_base = nseg if sample else 1
            k_out = t.get("last", False)

            def k_unit(tt, kv):
                T_, nseg_, L_ = tt["T"], tt["nseg"], tt["L"]
                ncs_ = L_ // 64
                sample_ = tt.get("sample", False)
                slot, b_slot = ring_take(("i", 4 + kv))
                wv = unit_view(slot)
                kb, b_kb = pg()
                mm_fm(kb[:, 0:T_], wv, 0, 0, T_, [b_slot], [b_kb])
                rb, b_rb = pg()
                mm_fm(rb[:, 0:T_], wv, 1, 0, T_, [b_slot], [b_rb])
                ring_issue()
                views = []
                for s in range(nseg_):
                    c0 = (s * (2 + ncs_) + 2) * 64
                    for z in range(2):
                        views.append((KTz[z][z * 64:(z + 1) * 64, kv, c0:c0 + L_], (z * 64, (z + 1) * 64),
                                      (s * L_, (s + 1) * L_)))
                f32o = None
                if tt.get("last", False):
                    if sample_:
                        f32o = (okT_s[:, kv * T_:(kv + 1) * T_], (0, T_))
                    else:
                        f32o = (okT_p[:, kv * 128:(kv + 1) * 128], (T_ - 128, T_))
                rope_pair(kb[:, 0:T_], b_kb, rb[:, 0:T_], b_rb, T_, b_KT, views, f32_dma=f32o)

            if mode != "pre" and not t.get("kx_done"):
                for kv in range(NKV):
                    k_unit(t, kv)

            def xr_units(tt, us):
                T_, nseg_, L_, run_ = tt["T"], tt["nseg"], tt["L"], tt["run"]
                for u in us:
                    slot, b_slot = ring_take(("i", u))
                    wv = unit_view(slot)
                    for f in range(2):
                        j = 2 * u + f
                        bank, b_bank = pg()
                        mm_fm(bank[:, 0:T_], wv, f, 0, T_, [b_slot], [b_bank])
                        lru_a(run_, j, bank[:, 0:T_], b_bank, T_, nseg_, L_)
                    ring_issue()

            if not t.get("xr_done") and not t.get("kx_done"):
                xr_units(t, range(4))

            if mode == "pre":
                if t.get("kv_last"):
                    for kv in range(NKV):
                        slot, b_slot = ring_take(("i", 4 + kv))
                        wv = unit_view(slot)
                        kb, b_kb = pg()
                        mm_fm(kb[:, 0:128], wv, 0, T - 128, T, [b_slot], [b_kb])
                        rb, b_rb = pg()
                        mm_fm(rb[:, 0:128], wv, 1, T - 128, T, [b_slot], [b_rb])
                        ring_issue()
                        rope_pair(kb[:, 0:128], b_kb, rb[:, 0:128], b_rb, 128, b_KT,
                                  [(KTz[z][z * 64:(z + 1) * 64, kv, 0:128], (z * 64, (z + 1) * 64), (0, 128))
                                   for z in range(2)])
                    vb, b_vb = pg()

                    def vfn(e):
                        last = None
                        for k in range(8):
                            last = e.matmul(vb[:, 0:256], lhsT=xnT[:, k, T - 128:T], rhs=w_v_sb[:, k, :],
                                            start=(k == 0), stop=(k == 7))
                        return last
                    S.op("pe", vfn, reads=b_xnT + [b_wv], writes=[b_vb])
                    S.op("act", lambda e: e.activation(out=V_all[:, 0, :], in_=vb[:, 0:256], func=AF.Copy),
                         reads=[b_vb], writes=[b_V])
                if nxt is not None and nxt["mode"] == "pre":
                    stage_a(ti + 1)
                    lru_b_chains(run, T, nseg, L, fillers=[
                        (1, lambda: cast_some(1)), (3, lambda: cast_some(1)), (5, lambda: cast_some(1)),
                        (8, lambda: xr_units(nxt, [0, 1])), (9, lambda: cast_some(1)),
                        (10, lambda: xr_units(nxt, [2, 3])), (10, lambda: cast_some(1))])
                    nxt["xr_done"] = True
                else:
                    lru_b_chains(run, T, nseg, L)
                    if nxt is not None:
                        stage_a(ti + 1)
                    cast_some(5)
                return

            if DBG.get("stage", 99) <= 1 and mode != "pre":
                return
            if not sample:
                S.op("pool", lambda e: e.tensor_copy(out=V2[64:128, 1, :], in_=V_all[64:128, 0, :]),
                     reads=[b_V], writes=[b_V])

            if DBG.get("stage", 99) <= 2 and mode != "pre":
                return
            def v_units(tt):
                T_ = tt["T"]
                NG_ = T_ // 128
                sample_ = tt.get("sample", False)
                ob_ = tt["nseg"] if sample_ else 1
                ko_ = tt.get("last", False)
                for g in range(NG_):
                    vb, b_vb = pg()

                    def vfn(e, g=g, vb=vb):
                        last = None
                        for k in range(8):
                            last = e.matmul(vb[:, 0:256], lhsT=xnT[:, k, g * 128:(g + 1) * 128], rhs=w_v_sb[:, k, :],
                                            start=(k == 0), stop=(k == 7))
                        return last
                    S.op("pe", vfn, reads=b_xnT + [b_wv], writes=[b_vb])
                    S.op("act", lambda e, g=g, vb=vb: e.activation(out=V_all[:, ob_ + g, :], in_=vb[:, 0:256], func=AF.Copy),
                         reads=[b_vb], writes=[b_V])
                    if not sample_:
                        S.op("act", lambda e, g=g, vb=vb: e.activation(out=V2[0:64, ob_ + g, :], in_=vb[0:64, 0:256],
                                                                     func=AF.Copy), reads=[b_vb], writes=[b_V])
                        S.op("act", lambda e, g=g, vb=vb: e.activation(out=V2[64:128, ob_ + g + 1, :], in_=vb[64:128, 0:256],
                                                                     func=AF.Copy), reads=[b_vb], writes=[b_V])
                    if ko_ and (sample_ or g == NG_ - 1):
                        vo, b_vo = tf()
                        S.op("act", lambda e, vo=vo, vb=vb: e.activation(out=vo[:, 0:256], in_=vb[:, 0:256], func=AF.Copy),
                             reads=[b_vb], writes=[b_vo])
                        vdst = ov_s[g * 128:(g + 1) * 128, :] if sample_ else ov_p
                        S.dma("sp", vdst, vo[:, 0:256], reads=[b_vo], writes=[], final=True)

            if not t.get("vq_done"):
                v_units(t)
            if DBG.get("stage", 99) <= 3 and mode != "pre":
                return
            def q_unit(j, alloc=None):
                slot, b_slot = ring_take(("i", 8 + j))
                wv = unit_view(slot)
                qb, b_qb = pg()
                mm_fm(qb[:, 0:T], wv, 0, 0, T, [b_slot], [b_qb])
                rb, b_rb = pg()
                mm_fm(rb[:, 0:T], wv, 1, 0, T, [b_slot], [b_rb])
                ring_issue()
                rope_pair(qb[:, 0:T], b_qb, rb[:, 0:T], b_rb, T, b_QT[j], [(QT[:, j, 0:T], (0, 128), (0, T))], alloc=alloc)

            def yg_unit(u, alloc=None):
                alloc = alloc or tf
                slot, b_slot = ring_take(("i", 16 + u))
                wv = unit_view(slot)
                for f in range(2):
                    j = 2 * u + f
                    bank, b_bank = pg()
                    mm_fm(bank[:, 0:T], wv, f, 0, T, [b_slot], [b_bank])
                    gl, b_gl = alloc()
                    S.op("act", lambda e, bank=bank, gl=gl: e.activation(out=gl[:, 0:T], in_=bank[:, 0:T],
                                                                       func=AF.Gelu_apprx_tanh),
                         reads=[b_bank], writes=[b_gl])
                    S.op("dve", lambda e, j=j, gl=gl: e.tensor_tensor(out=rnn[:, j, 0:T], in0=rnn[:, j, 0:T],
                                                                      in1=gl[:, 0:T], op=ALU.mult),
                         reads=[b_rnn[j], b_gl], writes=[b_rnn[j]])
                ring_issue()

            def gr_unit(u, alloc=None):
                alloc = alloc or tf
                slot, b_slot = ring_take(("i", 20 + u))
                wv = unit_view(slot)
                for f in range(2):
                    j = 2 * u + f
                    bank, b_bank = pg()
                    mm_fm(bank[:, 0:T], wv, f, 0, T, [b_slot], [b_bank])
                    sg, b_sg = alloc()
                    S.op("act", lambda e, bank=bank, sg=sg: e.activation(out=sg[:, 0:T], in_=bank[:, 0:T], func=AF.Tanh,
                                                                       scale=0.5),
                         reads=[b_bank], writes=[b_sg])
                    S.op("dve", lambda e, j=j, sg=sg: e.scalar_tensor_tensor(out=rnn[:, j, 0:T], in0=sg[:, 0:T], scalar=1.0,
                                                                             in1=rnn[:, j, 0:T], op0=ALU.add, op1=ALU.mult),
                         reads=[b_rnn[j], b_sg], writes=[b_rnn[j]])
                ring_issue()

            if not t.get("vq_done"):
                for j in range(4):
                    q_unit(j)
            def _cs_next():
                if nxt is not None:
                    load_cs(nxt)
            lru_b_chains(run, T, nseg, L, fillers=[
                (1, lambda: q_unit(4, alloc=tg)), (3, lambda: q_unit(5, alloc=tg)),
                (5, lambda: q_unit(6, alloc=tg)), (6, lambda: q_unit(7, alloc=tg)),
                (7, _cs_next),
                (7, lambda: yg_unit(0, alloc=tg)), (8, lambda: yg_unit(1, alloc=tg)),
                (9, lambda: gr_unit(0, alloc=tg)), (10, lambda: gr_unit(1, alloc=tg))])
            yg_unit(2)
            yg_unit(3)
            gr_unit(2)
            gr_unit(3)

            if DBG.get("stage", 99) <= 5 and mode != "pre":
                return
            numf = PH[:, 0:2, :].rearrange("p b c -> p (b c)")
            denf = PH[:, 2:4, :].rearrange("p b c -> p (b c)")
            hb_first = t.get("first", False)
            for kv in range(NKV):
                ga_slot, ga_b_slot = ring_take(("i", 24 + kv))
                ga_wv = unit_view(ga_slot)
                sgs = []
                for a in range(2):
                    bank, b_bank = pg()
                    mm_fm(bank[:, 0:T], ga_wv, a, 0, T, [ga_b_slot], [b_bank])
                    sg, b_sg = tf()
                    S.op("act", lambda e, bank=bank, sg=sg: e.activation(out=sg[:, 0:T], in_=bank[:, 0:T], func=AF.Tanh,
                                                                       scale=0.5),
                         reads=[b_bank], writes=[b_sg])
                    sgs.append((sg, b_sg))
                ring_issue()

                def att_front(c, kv=kv):
                    s, lc = c // ncs, c % ncs
                    hp = lc % 2
                    hc = (s * ncs + lc) % 2
                    kcs = [(kt_col(s, lc), hp, 0), (kt_col(s, lc + 1), 1 - hp, 0), (kt_col(s, lc + 2), hc, 256)]
                    stb, b_stb = pg()

                    def sfn(e, kcs=kcs, stb=stb, c=c, kv=kv):
                        last = None
                        for (ktc, half, cb) in kcs:
                            for gp in range(2):
                                out = stb[half * 64:(half + 1) * 64, cb + gp * 128:cb + (gp + 1) * 128]
                                last = e.matmul(out, lhsT=KTz[gp][:, kv, ktc:ktc + 64],
                                                rhs=QT[:, 2 * kv:2 * kv + 2, c * 64:(c + 1) * 64],
                                                start=True, stop=True)
                        return last
                    S.op("pe", sfn, reads=[b_KT, b_QT[2 * kv], b_QT[2 * kv + 1]], writes=[b_stb])
                    pt, b_pt = tb()
                    if hb_first and lc < 2:
                        bias = prm[:, P_HB + lc:P_HB + lc + 1]
                        S.op("act", lambda e, stb=stb, pt=pt, bias=bias: e.activation(out=pt[:, 0:256], in_=stb[:, 0:256],
                                                                                     func=AF.Exp, scale=0.125, bias=bias),
                             reads=[b_stb, b_prm], writes=[b_pt])
                    else:
                        S.op("act", lambda e, stb=stb, pt=pt: e.activation(out=pt[:, 0:256], in_=stb[:, 0:256],
                                                                         func=AF.Exp, scale=0.125),
                             reads=[b_stb], writes=[b_pt])
                    oi = (c // 2) % 2
                    po = ptown[:, hc, oi, :]
                    b_po = b_ptown[hc][oi]
                    S.op("act", lambda e, stb=stb, po=po, hc=hc: e.activation(out=po[hc * 64:(hc + 1) * 64, :],
                                                                            in_=stb[hc * 64:(hc + 1) * 64, 256:512],
                                                                            func=AF.Exp, scale=0.125),
                         reads=[b_stb], writes=[b_po])
                    return dict(pt=pt, b_pt=b_pt, po=po, b_po=b_po, s=s, lc=lc, hp=hp, c=c)

                def att_back(inf, kv=kv):
                    pt, b_pt, po, b_po, s, lc, hp, c = (inf[k] for k in ('pt', 'b_pt', 'po', 'b_po', 's', 'lc', 'hp', 'c'))
                    if lc == 0:
                        vpair = V_all[:, hslot(s), :]
                    elif hp == 0:
                        vpair = V_all[:, own_base + (s * ncs + lc - 2) // 2, :]
                    else:
                        vpair = V2[:, own_base + (s * ncs + lc - 1) // 2, :]
                    vown = V_all[:, own_base + (s * ncs + lc) // 2, :]

                    def pvfn(e, pt=pt, po=po, vpair=vpair, vown=vown, c=c, kv=kv):
                        last = None
                        for tgt, is_den in ((numf, False), (denf, True)):
                            for gp in range(2):
                                out = tgt[gp * 64:(gp + 1) * 64, c * 128:(c + 1) * 128]
                                l0 = ones_b[:, :] if is_den else vpair[:, kv * 64:(kv + 1) * 64]
                                l1 = ones_b[:, :] if is_den else vown[:, kv * 64:(kv + 1) * 64]
                                e.matmul(out, lhsT=l0, rhs=pt[:, gp * 128:(gp + 1) * 128], start=True, stop=False)
                                last = e.matmul(out, lhsT=l1, rhs=po[:, gp * 128:(gp + 1) * 128], start=False, stop=True)
                        return last
                    S.op("pe", pvfn, reads=[b_pt, b_po, b_V, b_ident], writes=b_PH)

                pend = []
                for c in range(nch + 2):
                    if c < nch:
                        pend.append(att_front(c))
                    if c >= 2 or c >= nch:
                        if pend:
                            att_back(pend.pop(0))
                while pend:
                    att_back(pend.pop(0))
                if DBG.get("att", 9) <= 3:
                    continue
                for a in range(2):
                    j = 2 * kv + a
                    nv = numf[:, 0:nch * 128].rearrange("p (c a q) -> p c a q", a=2, q=64)[:, :, a, :]
                    dv = denf[:, 0:nch * 128].rearrange("p (c a q) -> p c a q", a=2, q=64)[:, :, a, :]
                    rec, b_rec = tf()
                    rec3 = rec[:, 0:T].rearrange("p (c q) -> p c q", q=64)
                    S.op("dve", lambda e, dv=dv, rec3=rec3, j=j: e.tensor_scalar(out=rec3, in0=dv,
                                                                                  scalar1=cst[:, C_ES + j:C_ES + j + 1],
                                                                                  scalar2=2.0, op0=ALU.add, op1=ALU.mult),
                         reads=b_PH + [b_cst], writes=[b_rec])
                    S.op("dve", lambda e, rec=rec: e.reciprocal(out=rec[:, 0:T], in_=rec[:, 0:T]),
                         reads=[b_rec], writes=[b_rec])
                    S.op("dve", lambda e, nv=nv, rec3=rec3: e.tensor_tensor(out=rec3, in0=nv, in1=rec3, op=ALU.mult),
                         reads=b_PH + [b_rec], writes=[b_rec])
                    sg, b_sg = sgs[a]
                    S.op("dve", lambda e, sg=sg, rec=rec: e.scalar_tensor_tensor(out=rec[:, 0:T], in0=sg[:, 0:T], scalar=1.0,
                                                                                 in1=rec[:, 0:T], op0=ALU.add, op1=ALU.mult),
                         reads=[b_sg, b_rec], writes=[b_rec])
                    S.op("dve", lambda e, j=j, rec=rec: e.scalar_tensor_tensor(out=mixedT[:, j, 0:T], in0=rnn[:, j, 0:T],
                                                                               scalar=0.5, in1=rec[:, 0:T],
                                                                               op0=ALU.mult, op1=ALU.add),
                         reads=[b_rnn[j], b_rec], writes=[b_mix[j]])

            if DBG.get("stage", 99) <= 6 and mode != "pre":
                return
            if not sample:
                for z in range(2):
                    S.op("pool", lambda e, z=z: e.tensor_copy(out=KTz[z][:, :, 0:128],
                                                              in_=KTz[z][:, :, nch * 64:nch * 64 + 128]),
                         reads=[b_KT], writes=[b_KT])
                S.op("pool", lambda e: e.tensor_copy(out=V_all[:, 0, :], in_=V_all[:, NG, :]),
                     reads=[b_V], writes=[b_V])
            if DBG.get("stage", 99) <= 7 and mode != "pre":
                return
            wo = [ring_take(("o", u)) for u in range(4)]
            wov = [w_[0].rearrange("p (k c) -> p k c", k=2) for w_ in wo]
            for g in range(NG):
                hp = (g % 2) * 2

                def wofn(e, g=g, hp=hp):
                    last = None
                    for hh in range(2):
                        for k in range(8):
                            last = e.matmul(PH[:, hp + hh, :], lhsT=mixedT[:, k, g * 128:(g + 1) * 128],
                                            rhs=wov[k // 2][:, k % 2, hh * 512:(hh + 1) * 512], start=(k == 0), stop=(k == 7))
                    return last
                S.op("pe", wofn, reads=b_mix + [w_[1] for w_ in wo], writes=[b_PH[hp], b_PH[hp + 1]])
                for hh in range(2):
                    S.op("dve", lambda e, g=g, hp=hp, hh=hh: e.tensor_tensor(out=x_sb[:, g, hh * 512:(hh + 1) * 512],
                                                                             in0=x_sb[:, g, hh * 512:(hh + 1) * 512],
                                                                             in1=PH[:, hp + hh, :], op=ALU.add),
                         reads=[bx[g], b_PH[hp + hh]], writes=[bx[g]])
            if DBG.get("stage", 99) <= 8 and mode != "pre":
                return
            for _ in range(4):
                ring_issue()
            norm_T(x_sb, bx, T, P_G2)
            if DBG.get("stage", 99) <= 9 and mode != "pre":
                return
            fs = ffnst[run]

            def ffn_back(f, accs):
                (ag, b_ag), (av, b_av) = accs
                S.op("act", lambda e, ag=ag: e.activation(out=ag[:, 0:T], in_=ag[:, 0:T], func=AF.Silu),
                     reads=[b_ag], writes=[b_ag])
                S.op("dve", lambda e, ag=ag, av=av, f=f: e.tensor_tensor(out=hT[:, f, 0:T], in0=ag[:, 0:T], in1=av[:, 0:T],
                                                                         op=ALU.mult),
                     reads=[b_ag, b_av], writes=b_hT_all)

            ffn_prev = None
            for f in range(NF):
                slot, b_slot = ring_take(("u", f))
                wv = unit_view(slot)
                accs = []
                for tt in range(2):
                    fi = 2 * f + tt
                    bank, b_bank = pg()
                    mm_fm(bank[:, 0:T], wv, tt, 0, T, [b_slot], [b_bank])
                    if mode == "warm":
                        S.op("act", lambda e, bank=bank, fi=fi: e.activation(
                            out=fs[:, fi, :, :], in_=bank[:, 0:T].rearrange("p (s l) -> p s l", l=L)[:, :, L - 2:L],
                            func=AF.Copy), reads=[b_bank], writes=[b_ffnst[run]])
                        continue
                    W2 = 2 + L
                    up, b_up = tf()
                    up3 = up[:, 0:nseg * W2].rearrange("p (s c) -> p s c", c=W2)
                    S.op("act", lambda e, bank=bank, up3=up3: e.activation(
                        out=up3[:, :, 2:2 + L], in_=bank[:, 0:T].rearrange("p (s l) -> p s l", l=L), func=AF.Copy),
                         reads=[b_bank], writes=[b_up])
                    S.op("pool", lambda e, up3=up3, fi=fi: e.tensor_copy(out=up3[:, :, 0:2], in_=fs[:, fi, :, :]),
                         reads=[b_ffnst[run]], writes=[b_up])
                    acc, b_acc = tf()
                    acc3 = acc[:, 0:T].rearrange("p (s l) -> p s l", l=L)
                    fw = lambda i, fi=fi: prm[:, P_FW + 3 * fi + i:P_FW + 3 * fi + i + 1]
                    S.op("act", lambda e, acc3=acc3, bank=bank, fi=fi, fw=fw: e.activation(
                        out=acc3, in_=bank[:, 0:T].rearrange("p (s l) -> p s l", l=L), func=AF.Identity,
                        scale=fw(2), bias=prm[:, P_FB + fi:P_FB + fi + 1]), reads=[b_bank, b_prm], writes=[b_acc])
                    for i in (0, 1):
                        S.op("dve", lambda e, acc3=acc3, up3=up3, i=i, fw=fw: e.scalar_tensor_tensor(
                            out=acc3, in0=up3[:, :, i:i + L], scalar=fw(i), in1=acc3, op0=ALU.mult, op1=ALU.add),
                             reads=[b_up, b_prm, b_acc], writes=[b_acc])
                    S.op("pool", lambda e, up3=up3, fi=fi: e.tensor_copy(out=fs[:, fi, :, :], in_=up3[:, :, L:L + 2]),
                         reads=[b_up], writes=[b_ffnst[run]])
                    accs.append((acc, b_acc))
                ring_issue()
                if mode == "warm":
                    continue
                if ffn_prev is not None:
                    ffn_back(*ffn_prev)
                ffn_prev = (f, accs)
            if ffn_prev is not None:
                ffn_back(*ffn_prev)
            if nxt is not None:
                stage_a(ti + 1)
            if mode == "warm":
                return
            def wd_back(j, dl, b_dl):
                hb_ = j % 4
                tpf = PH[:, hb_, :].rearrange("p (g c) -> p g c", c=128)

                def tfn(e, dl=dl, tpf=tpf):
                    last = None
                    for g in range(NG):
                        last = e.transpose(out=tpf[:, g, :], in_=dl[:, g * 128:(g + 1) * 128], identity=ident_f[:])
                    return last
                S.op("pe", tfn, reads=[b_dl, b_ident], writes=[b_PH[hb_]])
                S.op("dve", lambda e, j=j, tpf=tpf: e.tensor_tensor(out=x_sb[:, 0:NG, j * 128:(j + 1) * 128],
                                                                   in0=x_sb[:, 0:NG, j * 128:(j + 1) * 128],
                                                                   in1=tpf[:, 0:NG, :], op=ALU.add),
                     reads=bx[:NG] + [b_PH[hb_]], writes=bx[:NG])

            wd_fill = []
            if nxt is not None and kx_early(tiles, ti + 1):
                wd_fill = [(lambda kv=kv: k_unit(nxt, kv)) for kv in range(NKV)] + \
                          [(lambda u=u: xr_units(nxt, [u])) for u in range(4)]
                nxt["kx_done"] = True
            wd_prev = None
            for j in range(8):
                bank, b_bank = pg()
                for hh in range(2):
                    slot, b_slot = ring_take(("d", 2 * j + hh))
                    wd = slot[:, 0:1408].rearrange("p (k c) -> p k c", c=128)

                    def dfn(e, hh=hh, wd=wd, bank=bank):
                        last = None
                        for kk in range(11):
                            k = hh * 11 + kk
                            last = e.matmul(bank[:, 0:T], lhsT=wd[:, kk, :], rhs=hT[:, k, 0:T],
                                            start=(k == 0), stop=(k == 21))
                        return last
                    S.op("pe", dfn, reads=[b_slot] + b_hT_all, writes=[b_bank])
                    ring_issue()
                dl, b_dl = tf()
                S.op("act", lambda e, bank=bank, dl=dl: e.activation(out=dl[:, 0:T], in_=bank[:, 0:T], func=AF.Copy),
                     reads=[b_bank], writes=[b_dl])
                if wd_prev is not None:
                    wd_back(*wd_prev)
                wd_prev = (j, dl, b_dl)
                if wd_fill:
                    wd_fill.pop(0)()
            wd_back(*wd_prev)
            if nxt is not None and kx_early(tiles, ti + 1):
                v_units(nxt)
                for j in range(4):
                    q_unit(j)
                nxt["vq_done"] = True
            yd = ydst[t["ydst"]]
            for g in range(NG):
                S.op("act", lambda e, g=g: e.activation(out=xn_tok[:, g, :], in_=x_sb[:, g, :], func=AF.Square,
                                                        accum_out=stat[:, g:g + 1]),
                     reads=[bx[g]], writes=[b_xn_tok[g], b_stat], noembed=True)
            S.op("dve", lambda e: e.tensor_scalar(out=stat[:, 4:4 + NG], in0=stat[:, 0:NG], scalar1=1.0 / D,
                                                  scalar2=EPS, op0=ALU.mult, op1=ALU.add),
                 reads=[b_stat], writes=[b_stat])
            S.op("act", lambda e: e.activation(out=stat[:, 12:12 + NG], in_=stat[:, 4:4 + NG], func=AF.Sqrt),
                 reads=[b_stat], writes=[b_stat])
            S.op("dve", lambda e: e.reciprocal(out=stat[:, 8:8 + NG], in_=stat[:, 12:12 + NG]),
                 reads=[b_stat], writes=[b_stat])
            for g in range(NG):
                yi = 0
                cnt["yst"] += 1
                S.op("dve", lambda e, g=g, yi=yi: e.scalar_tensor_tensor(out=ystage[:, yi, :], in0=x_sb[:, g, :],
                                                                         scalar=stat[:, 8 + g:9 + g], in1=gF[:],
                                                                         op0=ALU.mult, op1=ALU.mult),
                     reads=[bx[g], b_stat, b_gF], writes=[b_yst[yi]])
                r0 = t["row0"] + g * 128
                S.dma("sp", yd[r0:r0 + 128, :], ystage[:, yi, :], reads=[b_yst[yi]], writes=[], final=True)

        load_x(0)
        stage_a(0)
        for ti, t in enumerate(tiles):
            if ti >= DBG.get("ntiles", 10 ** 9):
                break
            emit_tile(ti, t)
            if t.get("post_flag"):
                fl = prm[:, P_FLAG:P_FLAG + 1]
                S.op("pool", lambda e: e.tensor_scalar(out=hst_p[:], in0=hst_p[:], scalar1=fl, scalar2=None, op0=ALU.mult),
                     reads=b_hst["p"] + [b_prm], writes=b_hst["p"])
                S.op("pool", lambda e: e.tensor_scalar(out=convst_p[:].rearrange("p j s r -> p (j s r)"),
                                                       in0=convst_p[:].rearrange("p j s r -> p (j s r)"),
                                                       scalar1=fl, scalar2=None, op0=ALU.mult),
                     reads=[b_convst["p"], b_prm], writes=[b_convst["p"]])
                S.op("pool", lambda e: e.tensor_scalar(out=ffnst_p[:].rearrange("p f s r -> p (f s r)"),
                                                       in0=ffnst_p[:].rearrange("p f s r -> p (f s r)"),
                                                       scalar1=fl, scalar2=None, op0=ALU.mult),
                     reads=[b_ffnst["p"], b_prm], writes=[b_ffnst["p"]])
            if t["mode"] == "full" and t.get("last"):
                if t.get("sample"):
                    S.dma("sp", oh_s, hst_s[:].rearrange("p j s -> p (j s)"), reads=b_hst["s"], writes=[], final=True)
                    S.dma("sp", oconv_s, convst_s[:].rearrange("p j s r -> p (j s r)"), reads=[b_convst["s"]],
                          writes=[], final=True)
                    S.dma("sp", offn_s, ffnst_s[:].rearrange("p f s r -> p (f s r)"), reads=[b_ffnst["s"]],
                          writes=[], final=True)
                else:
                    S.dma("sp", oh_p, hst_p[:].rearrange("p j s -> p (j s)"), reads=b_hst["p"], writes=[], final=True)
                    S.dma("sp", oconv_p, convst_p[:].rearrange("p j s r -> p (j s r)"), reads=[b_convst["p"]],
                          writes=[], final=True)
                    S.dma("sp", offn_p, ffnst_p[:].rearrange("p f s r -> p (f s r)"), reads=[b_ffnst["p"]],
                          writes=[], final=True)
        assert "ntiles" in DBG or rs["next_use"] == len(units), (rs, len(units))
        S.emit()
    return nc


def _rope_tables(positions):
    half = 8
    inv_freq = np.power(np.float32(THETA), -np.arange(half, dtype=np.float32) * np.float32(2.0 / 16)).astype(np.float32)
    ang = positions.astype(np.float32)[None, :] * inv_freq[:, None]
    cos = np.cos(ang).astype(np.float32)
    sin = np.sin(ang).astype(np.float32)
    npos = positions.shape[0]
    Cd = np.ones((64, npos), np.float32)
    Sd = np.zeros((64, npos), np.float32)
    Cd[0:8] = cos
    Cd[8:16] = cos
    Sd[0:8] = -sin
    Sd[8:16] = sin
    return np.concatenate([Cd, Cd], 0), np.concatenate([Sd, Sd], 0)


def _fm(v, n):
    return np.ascontiguousarray(v.reshape(n, 128).T)


def _prep_weights(inp):
    w_in = inp["w_in"][0]
    perm = np.arange(64)
    perm[0:8] = np.arange(8, 16)
    perm[8:16] = np.arange(0, 8)

    def cols_fc(cols):
        return w_in[:, cols].reshape(8, 128, 128).transpose(1, 0, 2)

    units = []
    for u in range(4):
        units.append([cols_fc(np.arange(C_XR + (2 * u + f) * 128, C_XR + (2 * u + f + 1) * 128)) for f in range(2)])
    for kv in range(4):
        base = C_K + kv * 64
        kc = np.concatenate([base + np.arange(64)] * 2)
        kr = np.concatenate([base + perm] * 2)
        units.append([cols_fc(kc), cols_fc(kr)])
    for j in range(8):
        qc = C_Q + j * 128 + np.arange(128)
        qr = np.concatenate([C_Q + j * 128 + perm, C_Q + j * 128 + 64 + perm])
        units.append([cols_fc(qc), cols_fc(qr)])
    for base in (C_YG, C_GR, C_GA):
        for u in range(4):
            units.append([cols_fc(np.arange(base + (2 * u + f) * 128, base + (2 * u + f + 1) * 128)) for f in range(2)])
    w_in_a = np.stack([np.stack(u, 1) for u in units], 0)
    w_in_a = np.ascontiguousarray(w_in_a.reshape(28, 128, 2048))
    w_v_a = np.ascontiguousarray(w_in[:, C_V:C_V + 256].reshape(8, 128, 256).transpose(1, 0, 2).reshape(128, 2048))

    w_up = inp["w_up"][0]
    uu = []
    for f in range(NF):
        pair = []
        for tt in range(2):
            cols = np.arange(tt * DFF + f * 128, tt * DFF + (f + 1) * 128)
            pair.append(w_up[:, cols].reshape(8, 128, 128).transpose(1, 0, 2))
        uu.append(np.stack(pair, 1))
    w_up_a = np.ascontiguousarray(np.stack(uu, 0).reshape(NF, 128, 2048))

    w_dn = inp["w_down"][0]
    wd = w_dn.reshape(NF, 128, 8, 128)
    dd = []
    for j in range(8):
        for hh in range(2):
            dd.append(wd[hh * 11:(hh + 1) * 11, :, j, :].transpose(1, 0, 2).reshape(128, 1408))
    w_dn_a = np.ascontiguousarray(np.stack(dd, 0))

    w_out_a = np.ascontiguousarray(inp["w_out"][0].reshape(8, 128, D).transpose(1, 0, 2).reshape(128, 8 * D))

    gates = np.zeros((128, 8, 2, 128), np.float32)
    for gi, key in enumerate(("lru_gate_a_w", "lru_gate_i_w")):
        w = inp[key][0]
        for j in range(8):
            for b in range(2):
                gates[b * 64:(b + 1) * 64, j, gi, b * 64:(b + 1) * 64] = w[2 * j + b]
    gates_a = gates.reshape(128, 8 * 2 * 128)

    prm = np.zeros((128, NPRM), np.float32)
    prm[:, P_G1:P_G1 + 8] = _fm(inp["norm_mix"][0], 8)
    cw = inp["lru_conv_w"][0]
    for i in range(4):
        prm[:, P_CW + i:P_CW + 32:4] = _fm(cw[i], 8)
    prm[:, P_CB:P_CB + 8] = _fm(inp["lru_conv_b"][0], 8)
    prm[:, P_BA:P_BA + 8] = _fm(inp["lru_gate_a_b"][0], 8)
    prm[:, P_BI:P_BI + 8] = _fm(inp["lru_gate_i_b"][0], 8)
    prm[:, P_LAM:P_LAM + 8] = _fm(inp["lru_lambda"][0], 8)
    sinks = inp["attn_sinks"][0]
    prm[:, P_SINK:P_SINK + 8] = _fm(np.repeat(sinks, 64), 8)
    prm[:, P_G2:P_G2 + 8] = _fm(inp["norm_ffn"][0], 8)
    fw = inp["ffn_conv_w"][0]
    fb = inp["ffn_conv_b"][0]
    for f in range(NF):
        for tt in range(2):
            fi = 2 * f + tt
            sl = slice(tt * DFF + f * 128, tt * DFF + (f + 1) * 128)
            for i in range(3):
                prm[:, P_FW + 3 * fi + i] = fw[i, sl]
            prm[:, P_FB + fi] = fb[sl]
    gF = np.ascontiguousarray(np.broadcast_to(inp["norm_final"][None, :], (128, D))).astype(np.float32)
    return dict(w_in_a=w_in_a, w_v_a=w_v_a, w_up_a=w_up_a, w_dn_a=w_dn_a, w_out_a=w_out_a, gates_a=gates_a,
                gF=gF), prm


def make_in_maps(inp, cfg, nb=NB):
    shared, prm0 = _prep_weights(inp)
    HALF = cfg.HALF
    in_maps = []
    for c in range(2 * nb):
        b, h = c // 2, c % 2
        m = dict(shared)
        xp = inp["x_prompt"][b]
        m["xmain"] = np.ascontiguousarray(xp[h * HALF:(h + 1) * HALF])
        m["xpre"] = np.ascontiguousarray(xp[0:HALF]) if h == 1 else np.zeros((HALF, D), np.float32)
        s0 = c * SPC
        m["xs"] = np.ascontiguousarray(inp["x_sample"][s0:s0 + SPC].reshape(SPC * LS, D))
        base = h * HALF
        pos = np.concatenate([np.maximum(base - 256 + np.arange(256), 0), base + np.arange(HALF),
                              PAST + np.arange(LS)]).astype(np.int64)
        Ct, St = _rope_tables(pos)
        m["ropeC"], m["ropeS"] = Ct, St
        prm = prm0.copy()
        prm[:, P_FLAG] = float(h)
        if h == 0:
            prm[:, P_HB] = MASK_NEG
            prm[64:128, P_HB + 1] = MASK_NEG
        m["prm"] = prm
        sh = inp["state_lru_h"][0, s0:s0 + SPC]
        m["s_h"] = np.ascontiguousarray(sh.reshape(SPC, 8, 128).transpose(2, 1, 0).reshape(128, 8 * SPC))
        sc = inp["state_lru_conv"][0, s0:s0 + SPC]
        m["s_conv"] = np.ascontiguousarray(sc.reshape(SPC, 3, 8, 128).transpose(3, 2, 0, 1).reshape(128, 8 * SPC * 3))
        sf = inp["state_ffn_conv"][0, s0:s0 + SPC]
        sf = sf.reshape(SPC, 2, 2, NF, 128)
        m["s_ffn"] = np.ascontiguousarray(sf.transpose(4, 3, 2, 0, 1).reshape(128, 44 * SPC * 2))
        ck = inp["cache_attn_k"][0, s0:s0 + SPC]
        kT = ck.transpose(3, 2, 0, 1)
        m["s_kT"] = np.ascontiguousarray(np.concatenate([kT, kT], 0).reshape(128, NKV * SPC * 128))
        m["s_k"] = np.ascontiguousarray(ck.reshape(SPC, 128, 256))
        m["s_v"] = np.ascontiguousarray(inp["cache_attn_v"][0, s0:s0 + SPC].reshape(SPC, 128, 256))
        in_maps.append(m)
    return in_maps


def assemble(results, cfg, nb=NB):
    HALF = cfg.HALF
    seq = 2 * HALF
    ns = 2 * nb * SPC
    y_p = np.zeros((nb, seq, D), np.float32)
    y_s = np.zeros((ns, LS, D), np.float32)
    k_p = np.zeros((1, nb, 128, NKV, 64), np.float32)
    v_p = np.zeros((1, nb, 128, NKV, 64), np.float32)
    h_p = np.zeros((1, nb, D), np.float32)
    lc_p = np.zeros((1, nb, 3, D), np.float32)
    fc_p = np.zeros((1, nb, 2, 2 * DFF), np.float32)
    k_s = np.zeros((1, ns, 128, NKV, 64), np.float32)
    v_s = np.zeros((1, ns, 128, NKV, 64), np.float32)
    h_s = np.zeros((1, ns, D), np.float32)
    lc_s = np.zeros((1, ns, 3, D), np.float32)
    fc_s = np.zeros((1, ns, 2, 2 * DFF), np.float32)
    for c, r in enumerate(results):
        b, h = c // 2, c % 2
        y_p[b, h * HALF:(h + 1) * HALF] = r["y_main"]
        s0 = c * SPC
        y_s[s0:s0 + SPC] = r["y_s"].reshape(SPC, LS, D)
        if h == 1:
            kT = r["okT_p"].reshape(128, NKV, 128)[0:64]
            k_p[0, b] = kT.transpose(2, 1, 0)
            v_p[0, b] = r["ov_p"].reshape(128, NKV, 64)
            h_p[0, b] = r["oh_p"].reshape(128, 8).T.reshape(D)
            lc_p[0, b] = r["oconv_p"].reshape(128, 8, 3).transpose(2, 1, 0).reshape(3, D)
            fc_p[0, b] = r["offn_p"].reshape(128, NF, 2, 2).transpose(3, 2, 1, 0).reshape(2, 2 * DFF)
        kTs = r["okT_s"].reshape(128, NKV, SPC, LS)[0:64]
        k_s[0, s0:s0 + SPC, 0:64] = r["okold_s"].reshape(SPC, 64, NKV, 64)
        k_s[0, s0:s0 + SPC, 64:128] = kTs.transpose(2, 3, 1, 0)
        v_s[0, s0:s0 + SPC, 0:64] = r["ovold_s"].reshape(SPC, 64, NKV, 64)
        v_s[0, s0:s0 + SPC, 64:128] = r["ov_s"].reshape(SPC, LS, NKV, 64)
        h_s[0, s0:s0 + SPC] = r["oh_s"].reshape(128, 8, SPC).transpose(2, 1, 0).reshape(SPC, D)
        lc_s[0, s0:s0 + SPC] = r["oconv_s"].reshape(128, 8, SPC, 3).transpose(2, 3, 1, 0).reshape(SPC, 3, D)
        fc_s[0, s0:s0 + SPC] = r["offn_s"].reshape(128, NF, 2, SPC, 2).transpose(3, 4, 2, 1, 0).reshape(SPC, 2, 2 * DFF)
    return (y_p, y_s, k_p, v_p, h_p, lc_p, fc_p, k_s, v_s, h_s, lc_s, fc_s)


_NC_CACHE = {}


def kernel(**inputs):
    inp = {k: np.asarray(v) for k, v in inputs.items()}
    half = inp["x_prompt"].shape[1] // 2
    nb = inp["x_prompt"].shape[0]
    cfg = Cfg(half)
    if half not in _NC_CACHE:
        _NC_CACHE[half] = build_program(cfg)
    nc = _NC_CACHE[half]
    in_maps = make_in_maps(inp, cfg, nb)
    res = run_bass_kernel_spmd(nc, in_maps, core_ids=list(range(2 * nb)))
    return assemble(res.results, cfg, nb)
```
